# Optimizing a Trainium2 kernel written in Bass

```python
import math, functools
import jax, jax.numpy as jnp
from jax import lax
import numpy as np

D_MODEL = 1024
BATCH = 8
SEQ = 2048
DEPTH = 1
DEC_BATCH = 32
DEC_SEQ = 4
PAST_LEN = 8192
PAGE_SIZE = 128

A_HEADS = 8
A_HEAD_DIM = 64
A_QK = A_HEADS * 2 * A_HEAD_DIM
A_V = A_HEADS * 2 * A_HEAD_DIM
Q_BLOCK = 128

M_HEADS = 4
M_HEAD_DIM = 256
M_WIDTH = M_HEADS * M_HEAD_DIM
M_CONV = 4
M_CHUNK = 64

P_HEADS = 8
P_NKEYS = 128
P_EXPERTS = P_NKEYS * P_NKEYS
P_KEY_DIM = 256
P_TOPK = 16
P_TOKEN_BLOCK = 256

ALPHA = (2.0 * DEPTH) ** 0.25
BETA = (8.0 * DEPTH) ** -0.25
LN_EPS = 1e-5
PAD_LOG_INPUT_GATE = -1e30

SPLIT_SIZES = (A_QK, A_QK, A_V, M_WIDTH, M_WIDTH, M_WIDTH, M_HEADS, M_HEADS, D_MODEL, D_MODEL)
SPLIT_POINTS = tuple(sum(SPLIT_SIZES[:i + 1]) for i in range(len(SPLIT_SIZES) - 1))
D_IN = sum(SPLIT_SIZES)
F_GATE_OFFSET = SPLIT_POINTS[6]

kernel_name = 'hybrid_diffattn_mlstm_peer_step'


def layer_norm(x, g, b):
    xf = x.astype(jnp.float32)
    mu = jnp.mean(xf, -1, keepdims=True)
    var = jnp.mean(jnp.square(xf - mu), -1, keepdims=True)
    return ((xf - mu) * lax.rsqrt(var + LN_EPS) * g.astype(jnp.float32) + b.astype(jnp.float32)).astype(x.dtype)


def rms_norm(x, g):
    xf = x.astype(jnp.float32)
    r = lax.rsqrt(jnp.mean(jnp.square(xf), -1, keepdims=True) + LN_EPS)
    return (xf * r * g.astype(jnp.float32)).astype(x.dtype)


def head_layer_norm(h, g):
    mu = jnp.mean(h, -1, keepdims=True)
    var = jnp.mean(jnp.square(h - mu), -1, keepdims=True)
    return (h - mu) * lax.rsqrt(var + LN_EPS) * g.astype(jnp.float32).reshape(h.shape[-2:])


def diff_attn_core(q, k, v, q_pos, k_pos, lam):
    s = jnp.einsum('bqhcd,bkhcd->bhcqk', q, k).astype(jnp.float32) * (A_HEAD_DIM ** -0.5)
    mask = k_pos[None, :] <= q_pos[:, None]
    p = jax.nn.softmax(jnp.where(mask, s, -jnp.inf), axis=-1)
    a = p[:, :, 0] - lam * p[:, :, 1]
    return jnp.einsum('bhqk,bkhe->bqhe', a.astype(v.dtype), v)


def attend_prompt(q, k, v, lam):
    B, S = q.shape[0], q.shape[1]
    k_pos = jnp.arange(S)

    def one_block(i):
        qb = lax.dynamic_slice_in_dim(q, i * Q_BLOCK, Q_BLOCK, axis=1)
        q_pos = i * Q_BLOCK + jnp.arange(Q_BLOCK)
        return diff_attn_core(qb, k, v, q_pos, k_pos, lam)

    out = lax.map(one_block, jnp.arange(S // Q_BLOCK))
    return jnp.moveaxis(out, 0, 1).reshape(B, S, A_HEADS, 2 * A_HEAD_DIM)


def attend_sample(q, k, v, lam, past_k, past_v):
    B, S = q.shape[0], q.shape[1]
    P = past_k.shape[1]
    k_all = jnp.concatenate([past_k.reshape(B, P, A_HEADS, 2, A_HEAD_DIM), k], axis=1)
    v_all = jnp.concatenate([past_v, v], axis=1)
    k_pos = jnp.arange(P + S)
    q_pos = P + jnp.arange(S)
    return diff_attn_core(q, k_all, v_all, q_pos, k_pos, lam)


def mlstm_chunkwise(q, k, v, logi, logf, C0, n0, m0):
    B, H, S, dh = q.shape
    L = min(M_CHUNK, S)
    pad = (-S) % L
    if pad:
        pw = ((0, 0), (0, 0), (0, pad), (0, 0))
        q, k, v = jnp.pad(q, pw), jnp.pad(k, pw), jnp.pad(v, pw)
        logi = jnp.pad(logi, pw[:3], constant_values=PAD_LOG_INPUT_GATE)
        logf = jnp.pad(logf, pw[:3], constant_values=0.0)
    nc = (S + pad) // L

    def to_chunks(a):
        return jnp.moveaxis(a.reshape((B, H, nc, L) + a.shape[3:]), 2, 0)

    tril = jnp.tril(jnp.ones((L, L), dtype=bool))

    def step(carry, inp):
        C, n, m = carry
        qc, kc, vc, ic, fc = inp
        b = jnp.cumsum(fc, axis=-1)
        D = jnp.where(tril, b[..., :, None] - b[..., None, :] + ic[..., None, :], -jnp.inf)
        m_t = jnp.maximum(b + m[..., None], jnp.max(D, axis=-1))
        W = jnp.exp(D - m_t[..., None])
        inter = jnp.exp(b + m[..., None] - m_t)
        sw = jnp.einsum('bhtd,bhsd->bhts', qc, kc) * W
        num = inter[..., None] * jnp.einsum('bhtd,bhed->bhte', qc, C) + jnp.einsum('bhts,bhse->bhte', sw, vc)
        dot = inter * jnp.einsum('bhtd,bhd->bht', qc, n) + jnp.sum(sw, axis=-1)
        h = num / jnp.maximum(jnp.abs(dot), jnp.exp(-m_t))[..., None]
        m_new = m_t[..., -1]
        decay = jnp.exp(b[..., -1] + m - m_new)
        w = jnp.exp(b[..., -1:] - b + ic - m_new[..., None])
        C_new = decay[..., None, None] * C + jnp.einsum('bhs,bhse,bhsd->bhed', w, vc, kc)
        n_new = decay[..., None] * n + jnp.einsum('bhs,bhsd->bhd', w, kc)
        return (C_new, n_new, m_new), h

    (C, n, m), hs = lax.scan(step, (C0, n0, m0), (to_chunks(q), to_chunks(k), to_chunks(v), to_chunks(logi), to_chunks(logf)))
    h = jnp.moveaxis(hs, 0, 2).reshape(B, H, nc * L, dh)[:, :, :S]
    return h, C, n, m


def token_mix(x, attend, C0, n0, m0, conv0, lam_init, w_in, b_in, lam_q1, lam_k1, lam_q2, lam_k2, subln_g,
              w_conv, b_conv, w_qm, w_km, mnorm_g, w_a, w_b, w_o):
    f32 = jnp.float32
    B, S, _ = x.shape
    proj = jnp.einsum('bsd,de->bse', x, w_in) + b_in
    qa, ka, va, u, vm, om, ig, fg, ga, gb = jnp.split(proj, SPLIT_POINTS, axis=-1)

    qa = qa.reshape(B, S, A_HEADS, 2, A_HEAD_DIM)
    k_rows = ka.reshape(B, S, A_HEADS, 2 * A_HEAD_DIM)
    v_rows = va.reshape(B, S, A_HEADS, 2 * A_HEAD_DIM)
    lam = (jnp.exp(jnp.sum(lam_q1.astype(f32) * lam_k1.astype(f32)))
           - jnp.exp(jnp.sum(lam_q2.astype(f32) * lam_k2.astype(f32))) + lam_init)
    oa = attend(qa, k_rows.reshape(B, S, A_HEADS, 2, A_HEAD_DIM), v_rows, lam)
    oa = rms_norm(oa, subln_g) * (1.0 - lam_init)
    ya = jnp.einsum('bse,ed->bsd', oa.reshape(B, S, A_V), w_a)

    u_ext = jnp.concatenate([conv0.astype(u.dtype), u], axis=1)
    uc = b_conv
    for j in range(M_CONV):
        uc = uc + u_ext[:, j:j + S] * w_conv[j]
    uc = jax.nn.silu(uc).reshape(B, S, M_HEADS, M_HEAD_DIM)
    new_conv = u_ext[:, u_ext.shape[1] - (M_CONV - 1):]
    qm = jnp.einsum('bshd,hde->bhse', uc, w_qm).astype(f32)
    km = (jnp.einsum('bshd,hde->bhse', uc, w_km) * (M_HEAD_DIM ** -0.5)).astype(f32)
    vmh = jnp.swapaxes(vm.reshape(B, S, M_HEADS, M_HEAD_DIM), 1, 2).astype(f32)
    logi = jnp.swapaxes(ig.astype(f32), 1, 2)
    logf = jax.nn.log_sigmoid(jnp.swapaxes(fg.astype(f32), 1, 2))
    h, C, n, m = mlstm_chunkwise(qm, km, vmh, logi, logf, C0.astype(f32), n0.astype(f32), m0.astype(f32))
    h = head_layer_norm(jnp.swapaxes(h, 1, 2), mnorm_g).reshape(B, S, M_WIDTH)
    yb = jnp.einsum('bse,ed->bsd', (jax.nn.sigmoid(om.astype(f32)) * h).astype(x.dtype), w_b)

    merged = jax.nn.sigmoid(ga) * ya + jax.nn.sigmoid(gb) * yb
    out = jnp.einsum('bse,ed->bsd', merged, w_o)
    return out, (k_rows, v_rows, C, n, m, new_conv)


def peer_ffn(x, w_pq, p_keys, p_u, p_v):
    B, S, D = x.shape
    T = B * S
    blk = min(P_TOKEN_BLOCK, T)
    pad = (-T) % blk
    xt = jnp.pad(x.reshape(T, D), ((0, pad), (0, 0)))

    def one_block(xb):
        q = jnp.einsum('td,de->te', xb, w_pq).reshape(blk, P_HEADS, 2, P_KEY_DIM // 2)
        s = jnp.einsum('thcd,hcnd->thcn', q, p_keys).astype(jnp.float32)
        sv, si = lax.top_k(s, P_TOPK)
        cand = (sv[:, :, 0, :, None] + sv[:, :, 1, None, :]).reshape(blk, P_HEADS, P_TOPK * P_TOPK)
        cidx = (si[:, :, 0, :, None] * P_NKEYS + si[:, :, 1, None, :]).reshape(blk, P_HEADS, P_TOPK * P_TOPK)
        top, pos = lax.top_k(cand, P_TOPK)
        eidx = jnp.take_along_axis(cidx, pos, axis=-1)
        g = jax.nn.softmax(top, axis=-1)
        ue = p_u[eidx]
        ve = p_v[eidx]
        a = jax.nn.gelu(jnp.einsum('td,thkd->thk', xb, ue).astype(jnp.float32), approximate=False)
        return jnp.einsum('thk,thkd->td', (g * a).astype(xb.dtype), ve)

    out = lax.map(one_block, xt.reshape(-1, blk, D))
    return out.reshape(-1, D)[:T].reshape(B, S, D)


def decoder_layer(x, attend, state, lam_init, mix_w, ln1_g, ln1_b, peer_w, ln2_g, ln2_b):
    C0, n0, m0, conv0 = state
    mix, new_state = token_mix(x, attend, C0, n0, m0, conv0, lam_init, *mix_w)
    h1 = layer_norm(ALPHA * x + mix, ln1_g, ln1_b)
    y = layer_norm(ALPHA * h1 + peer_ffn(h1, *peer_w), ln2_g, ln2_b)
    return y, new_state


def setup_inputs(seed: int = 0) -> dict:
    key = jax.random.key(seed)
    ks = list(jax.random.split(key, 40))
    f32 = jnp.float32

    def nrm(i, shape, scale):
        return jax.random.normal(ks[i], shape, f32) * scale

    n_pages = PAST_LEN // PAGE_SIZE
    n_pool = (DEC_BATCH * n_pages * 5) // 4
    page_table = jax.random.permutation(ks[0], n_pool)[:DEC_BATCH * n_pages].reshape(DEC_BATCH, n_pages).astype(jnp.int32)
    kv_shape = (DEPTH, n_pool, PAGE_SIZE, A_HEADS, 2 * A_HEAD_DIM)
    b_in = nrm(13, (DEPTH, D_IN), 0.02).at[:, F_GATE_OFFSET:F_GATE_OFFSET + M_HEADS].add(jnp.linspace(3.0, 6.0, M_HEADS))
    return {
        'x_prompt': nrm(1, (BATCH, SEQ, D_MODEL), 1.0),
        'x_sample': nrm(2, (DEC_BATCH, DEC_SEQ, D_MODEL), 1.0),
        'cache_k': nrm(3, kv_shape, 1.0),
        'cache_v': nrm(4, kv_shape, 1.0),
        'state_C': nrm(5, (DEPTH, DEC_BATCH, M_HEADS, M_HEAD_DIM, M_HEAD_DIM), 0.1),
        'state_n': nrm(6, (DEPTH, DEC_BATCH, M_HEADS, M_HEAD_DIM), 0.5),
        'state_m': nrm(7, (DEPTH, DEC_BATCH, M_HEADS), 0.5),
        'state_conv': nrm(8, (DEPTH, DEC_BATCH, M_CONV - 1, M_WIDTH), 1.0),
        'page_table': page_table,
        'w_in': nrm(9, (DEPTH, D_MODEL, D_IN), D_MODEL ** -0.5),
        'b_in': b_in,
        'lam_q1': nrm(10, (DEPTH, A_HEAD_DIM), 0.1),
        'lam_k1': nrm(11, (DEPTH, A_HEAD_DIM), 0.1),
        'lam_q2': nrm(12, (DEPTH, A_HEAD_DIM), 0.1),
        'lam_k2': nrm(14, (DEPTH, A_HEAD_DIM), 0.1),
        'subln_g': 1.0 + nrm(15, (DEPTH, 2 * A_HEAD_DIM), 0.02),
        'w_conv': nrm(16, (DEPTH, M_CONV, M_WIDTH), M_CONV ** -0.5),
        'b_conv': nrm(17, (DEPTH, M_WIDTH), 0.02),
        'w_qm': nrm(18, (DEPTH, M_HEADS, M_HEAD_DIM, M_HEAD_DIM), M_HEAD_DIM ** -0.5),
        'w_km': nrm(19, (DEPTH, M_HEADS, M_HEAD_DIM, M_HEAD_DIM), M_HEAD_DIM ** -0.5),
        'mnorm_g': 1.0 + nrm(20, (DEPTH, M_WIDTH), 0.02),
        'w_a': nrm(21, (DEPTH, A_V, D_MODEL), A_V ** -0.5),
        'w_b': nrm(22, (DEPTH, M_WIDTH, D_MODEL), M_WIDTH ** -0.5),
        'w_o': nrm(23, (DEPTH, D_MODEL, D_MODEL), BETA * D_MODEL ** -0.5),
        'ln1_g': 1.0 + nrm(24, (DEPTH, D_MODEL), 0.02),
        'ln1_b': nrm(25, (DEPTH, D_MODEL), 0.02),
        'w_pq': nrm(26, (DEPTH, D_MODEL, P_HEADS * P_KEY_DIM), D_MODEL ** -0.5),
        'p_keys': nrm(27, (DEPTH, P_HEADS, 2, P_NKEYS, P_KEY_DIM // 2), (P_KEY_DIM // 2) ** -0.5),
        'p_u': nrm(28, (DEPTH, P_EXPERTS, D_MODEL), D_MODEL ** -0.5),
        'p_v': nrm(29, (DEPTH, P_EXPERTS, D_MODEL), BETA * P_HEADS ** -0.5),
        'ln2_g': 1.0 + nrm(30, (DEPTH, D_MODEL), 0.02),
        'ln2_b': nrm(31, (DEPTH, D_MODEL), 0.02),
    }


def reference(x_prompt, x_sample, cache_k, cache_v, state_C, state_n, state_m, state_conv, page_table,
              w_in, b_in, lam_q1, lam_k1, lam_q2, lam_k2, subln_g, w_conv, b_conv, w_qm, w_km, mnorm_g,
              w_a, w_b, w_o, ln1_g, ln1_b, w_pq, p_keys, p_u, p_v, ln2_g, ln2_b):
    f32 = jnp.float32
    B = x_prompt.shape[0]
    DB = x_sample.shape[0]
    zero_state = (jnp.zeros((B, M_HEADS, M_HEAD_DIM, M_HEAD_DIM), f32),
                  jnp.zeros((B, M_HEADS, M_HEAD_DIM), f32),
                  jnp.zeros((B, M_HEADS), f32),
                  jnp.zeros((B, M_CONV - 1, M_WIDTH), x_prompt.dtype))
    hp, hs = x_prompt, x_sample
    new_p, new_s = [], []
    for l in range(DEPTH):
        lam_init = 0.8 - 0.6 * math.exp(-0.3 * l)
        mix_w = (w_in[l], b_in[l], lam_q1[l], lam_k1[l], lam_q2[l], lam_k2[l], subln_g[l], w_conv[l], b_conv[l],
                 w_qm[l], w_km[l], mnorm_g[l], w_a[l], w_b[l], w_o[l])
        peer_w = (w_pq[l], p_keys[l], p_u[l], p_v[l])
        past_k = cache_k[l][page_table].reshape(DB, -1, A_HEADS, 2 * A_HEAD_DIM)
        past_v = cache_v[l][page_table].reshape(DB, -1, A_HEADS, 2 * A_HEAD_DIM)
        hp, sp = decoder_layer(hp, attend_prompt, zero_state, lam_init, mix_w, ln1_g[l], ln1_b[l], peer_w, ln2_g[l], ln2_b[l])
        hs, ss = decoder_layer(hs, functools.partial(attend_sample, past_k=past_k, past_v=past_v),
                               (state_C[l], state_n[l], state_m[l], state_conv[l]), lam_init, mix_w,
                               ln1_g[l], ln1_b[l], peer_w, ln2_g[l], ln2_b[l])
        new_p.append(sp)
        new_s.append(ss)
    k_p, v_p, C_p, n_p, m_p, conv_p = [jnp.stack([st[i] for st in new_p]) for i in range(6)]
    k_s, v_s, C_s, n_s, m_s, conv_s = [jnp.stack([st[i] for st in new_s]) for i in range(6)]
    return (hp, hs, k_p, v_p, C_p, n_p, m_p, conv_p, k_s, v_s, C_s, n_s, m_s, conv_s)
```

```python
import numpy as np
from contextlib import ExitStack
import concourse.bass as bass
import concourse.mybir as mybir
from concourse.bass_utils import run_bass_kernel_spmd

F32 = mybir.dt.float32
BF16 = mybir.dt.bfloat16
I32 = mybir.dt.int32
U32 = mybir.dt.uint32
AF = mybir.ActivationFunctionType
ALU = mybir.AluOpType
AX = mybir.AxisListType

D = 1024
D_IN = 8200
NCORES = 8
EPOCH = 8000
DMA_RENEW = 500


class Cfg:
    S = 2048
    NCORES = 4
    NPS = 2
    NSQ = 8
    debug = False
    attn = True
    phaseS = True
    PAST = 8192
    NPOOL = 2560
    phaseD = True
    peer_only = False
    phaseC = True
    attn_c1 = True
    a_lam = True
    a_qk = True
    a_vx = True


class Buf:
    __slots__ = ("t", "name", "w", "r", "dsem", "dcnt", "dsems")

    def __init__(self, t, name):
        self.t = t
        self.name = name
        self.w = None
        self.r = []
        self.dsem = None
        self.dcnt = 0
        self.dsems = []

    def __getitem__(self, k):
        return self.t[k]


class Sched:
    def __init__(self, nc, es):
        self.nc = nc
        self.es = es
        self.eng = {"pe": nc.tensor, "dve": nc.vector, "act": nc.scalar, "pool": nc.gpsimd, "sp": nc.sync}
        self.cnt = {e: 0 for e in self.eng}
        self.sems = {e: [] for e in self.eng}
        self.seen = {e: {} for e in self.eng}
        self.nsem = 0
        self.out_events = []
        self.dma_events = {}
        self.scoped = {}
        self.free_dsems = []

    def new_sem(self, name):
        self.nsem += 1
        return self.es.enter_context(self.nc.semaphore(f"{name}_{self.nsem}"))

    def sbuf(self, name, shape, dt, es=None):
        self.nsem += 1
        name = f"{name}_u{self.nsem}"
        t = (es or self.es).enter_context(self.nc.sbuf_tensor(name, list(shape), dt))
        b = Buf(t, name)
        if es is not None:
            self.scoped.setdefault(id(es), []).append(b)
        return b

    def end_scope(self, es):
        for b in self.scoped.pop(id(es), []):
            if b.dsem is not None and b.dcnt < 16 * DMA_RENEW:
                self.free_dsems.append((b.dsem, b.dcnt))
            b.dsem = None

    def barrier(self):
        evs = list(self.dma_events.values())
        for e2 in self.eng:
            c = self.cnt[e2]
            if c:
                ep = (c - 1) // EPOCH
                evs.append((self.sems[e2][ep], c - ep * EPOCH))
        for e in self.eng:
            for ev in evs:
                self._wait(e, ev)
        self.dma_events = {}

    def psum(self, name, shape, dt):
        t = self.es.enter_context(self.nc.psum_tensor(name, list(shape), dt))
        return Buf(t, name)

    def dram(self, name, shape, dt, kind):
        t = self.nc.dram_tensor(name, list(shape), dt, kind=kind)
        return Buf(t.ap(), name)

    def _wait(self, e, ev):
        if ev is None:
            return
        sem, val = ev
        k = id(sem)
        if self.seen[e].get(k, 0) >= val:
            return
        self.eng[e].wait_ge(sem, val)
        self.seen[e][k] = val

    def _deps(self, e, reads, writes):
        for b in reads:
            self._wait(e, b.w)
        for b in writes:
            self._wait(e, b.w)
            for ev in b.r:
                self._wait(e, ev)

    def op(self, e, fn, reads=(), writes=(), chain=()):
        self._deps(e, reads, writes)
        writes = list(writes) + list(chain)
        ins = fn(self.eng[e])
        self.cnt[e] += 1
        c = self.cnt[e]
        ep = (c - 1) // EPOCH
        while len(self.sems[e]) <= ep:
            self.sems[e].append(self.new_sem(f"s{e}"))
        sem = self.sems[e][ep]
        val = c - ep * EPOCH
        ins.then_inc(sem, 1)
        ev = (sem, val)
        self.seen[e][id(sem)] = max(self.seen[e].get(id(sem), 0), 0)
        for b in reads:
            b.r.append(ev)
        for b in writes:
            b.w = ev
            b.r = []
        return ev

    def dma(self, q, out, in_, reads, writes, key, indirect=None, is_output=False, slow=False):
        self._deps(q, reads, writes)
        if key.dsem is None or key.dcnt >= 16 * DMA_RENEW:
            if self.free_dsems:
                key.dsem, key.dcnt = self.free_dsems.pop()
            else:
                key.dsem = self.new_sem(f"d{key.name}")
                key.dcnt = 0
        if indirect is None:
            ins = self.eng[q].dma_start(out=out, in_=in_, allow_slow_non_contiguous=True) if slow else self.eng[q].dma_start(out=out, in_=in_)
        else:
            ins = self.eng[q].indirect_dma_start(out=out, in_=in_, **indirect)
        key.dcnt += 16
        ins.then_inc(key.dsem, 16)
        ev = (key.dsem, key.dcnt)
        self.dma_events[id(key.dsem)] = ev
        for b in reads:
            b.r.append(ev)
        for b in writes:
            b.w = ev
            b.r = []
        if is_output:
            self.out_events.append(ev)
        return ev

    def collective(self, kind, op, groups, src, dst):
        self._deps("pool", [src], [dst])
        if dst.dsem is None:
            dst.dsem = self.new_sem(f"cc{dst.name}")
            dst.dcnt = 0
        ins = self.eng["pool"].collective_compute(kind, op, replica_groups=groups, ins=[src[:]], outs=[dst[:]])
        dst.dcnt += 16
        ins.then_inc(dst.dsem, 16)
        ev = (dst.dsem, dst.dcnt)
        self.dma_events[id(dst.dsem)] = ev
        src.r.append(ev)
        dst.w = ev
        dst.r = []
        return ev

    def finish(self):
        last = {}
        for sem, val in self.out_events:
            k = id(sem)
            if k not in last or last[k][1] < val:
                last[k] = (sem, val)
        for ev in last.values():
            self._wait("sp", ev)


LAM_INIT = 0.2
ALPHA = 2.0 ** 0.25
NCONST = 27


def make_consts(NSQ):
    c = np.zeros((NCONST, 128, 128), np.float32)
    k = np.arange(128)[:, None]
    t = np.arange(128)[None, :]
    nv = 4 * NSQ
    c[0] = np.eye(128)
    c[1] = 1.0
    c[2] = (k <= t)
    same = (k // 4 == t // 4) & (k < nv) & (t < nv)
    c[3] = (same & (k <= t)) | (k == t)
    c[4] = np.where(t <= k, 0.0, -1e30)
    c[5] = np.where((same & (t <= k)) | (k == t), 0.0, -1e30)
    c[6] = (k == 127) * np.ones((1, 128))
    for j in range(NSQ):
        c[7 + j] = (k == 4 * j + 3) * np.ones((1, 128))
    p = t
    c[15] = np.where(p < nv, k == 4 * (p // 4) + 3, k == p)
    for j in range(NSQ):
        c[16 + j] = ((t >= 4 * j) & (t < 4 * j + 4)) * np.ones((128, 1))
    c[24] = (np.arange(128) % 16)[None, :]
    c[25] = (np.arange(128) % 16)[None, :]
    c[26] = np.where(k <= (t % 4), 0.0, -30000.0)
    return np.ascontiguousarray(c.transpose(1, 0, 2))


def build(cfg):
    nc = bass.Bass("TRN2", target_bir_lowering=False)
    S = cfg.S
    NT = S // 128
    NPS, NSQ = cfg.NPS, cfg.NSQ
    T = S + 128
    TD = NPS * S + 128

    with ExitStack() as es:
        sc = Sched(nc, es)
        xin = sc.dram("xin", [TD, D], F32, "ExternalInput")
        w_in = sc.dram("w_in", [D, D_IN], F32, "ExternalInput")
        b_in = sc.dram("b_in", [1, D_IN], F32, "ExternalInput")
        consts_d = sc.dram("consts", [128, NCONST, 128], F32, "ExternalInput")
        smallc_d = sc.dram("smallc", [128, 128], F32, "ExternalInput")
        mst_d = sc.dram("mst", [128, 4 + 4 * NSQ], F32, "ExternalInput")
        wqm_d = sc.dram("w_qm", [4, 256, 256], F32, "ExternalInput")
        wkm_d = sc.dram("w_km", [4, 256, 256], F32, "ExternalInput")
        stC_d = sc.dram("st_C", [NSQ, 4, 256, 256], F32, "ExternalInput")
        stn_d = sc.dram("st_n", [NSQ, 4, 256], F32, "ExternalInput")
        convT_d = sc.dram("st_convT", [128, 8, NSQ, 3], F32, "ExternalInput")
        lam_d = sc.dram("lam", [4, 64], F32, "ExternalInput")
        SCR = "ExternalOutput" if cfg.debug else "Internal"
        oa_sc = sc.dram("oa_sc", [TD, D], F32, SCR)
        ma_sc = sc.dram("ma_sc", [TD, D], F32, SCR)
        h1_sc = sc.dram("h1_sc", [TD, D], F32, SCR)
        wpq_d = sc.dram("w_pq", [D, 2048], F32, "ExternalInput")
        pk_d = sc.dram("p_keys", [16, 128, 128], F32, "ExternalInput")
        pu_d = sc.dram("p_u", [16384, D], F32, "ExternalInput")
        pv_d = sc.dram("p_v", [16384, D], F32, "ExternalInput")
        y_out = sc.dram("y_out", [TD, D], F32, "ExternalOutput")
        if cfg.peer_only:
            h1in_d = sc.dram("h1_in", [TD, D], F32, "ExternalInput")
        if cfg.debug:
            dbg_ei = sc.dram("dbg_ei", [TD, 128], I32, "ExternalOutput")
            dbg_g = sc.dram("dbg_g", [TD, 128], F32, "ExternalOutput")
        NPP = cfg.PAST // 128
        ck_d = sc.dram("cache_k", [cfg.NPOOL * 128, D], F32, "ExternalInput")
        cv_d = sc.dram("cache_v", [cfg.NPOOL * 128, D], F32, "ExternalInput")
        ptab_d = sc.dram("ptab", [1, NSQ * NPP], I32, "ExternalInput")
        wa_d = sc.dram("w_a", [D, D], F32, "ExternalInput")
        wb_d = sc.dram("w_b", [D, D], F32, "ExternalInput")
        wo_d = sc.dram("w_o", [D, D], F32, "ExternalInput")
        subg_d = sc.dram("subln_g", [1, 128], F32, "ExternalInput")
        vec_d = sc.dram("vecs", [8, D], F32, "ExternalInput")
        k_out = sc.dram("k_out", [TD, D], F32, "ExternalOutput")
        v_out = sc.dram("v_out", [TD, D], F32, "ExternalOutput")
        conv_out = sc.dram("conv_out", [TD, D], F32, "ExternalOutput")
        h_out = sc.dram("h_out", [TD, D], F32, "ExternalOutput")
        C_out = sc.dram("C_out", [NPS + NSQ, 4, 256, 256], F32, "ExternalOutput")
        n_out = sc.dram("n_out", [NPS + NSQ, 4, 256], F32, "ExternalOutput")
        m_out = sc.dram("m_out", [NPS + NSQ, 4], F32, "ExternalOutput")

        cst = sc.sbuf("cst", [128, NCONST, 128], F32)
        smallc = sc.sbuf("smallc_sb", [128, 128], F32)
        xT = sc.sbuf("xT", [128, 8, T], BF16)
        ps = [sc.psum(f"ps{i}", [128, 512], F32) for i in range(8)]
        psi = [0]

        def nps():
            psi[0] += 1
            return ps[psi[0] % 8]

        def nps2():
            psi[0] += 1
            return ps[psi[0] % 2]

        sc.dma("sp", cst[:], consts_d[:], [consts_d], [cst], cst)
        sc.dma("sp", smallc[:], smallc_d[:], [smallc_d], [smallc], smallc)
        epsb = sc.sbuf("epsb", [128, 1], F32)
        sc.op("pool", lambda e: e.memset(epsb[:], 1e-5), writes=[epsb])
        epsc = epsb
        lams = sc.sbuf("lams", [128, 4], F32)
        with ExitStack() as e0:
            lamv = sc.sbuf("lamv", [128, 4, 64], F32, e0)
            lamw = sc.sbuf("lamw", [128, 2, 64], F32, e0)
            for i in range(4):
                sc.dma("sp", lamv[:, i, :], lam_d[i:i + 1, :].partition_broadcast(128), [lam_d], [lamv], lamv)
            sc.op("dve", lambda e: e.tensor_tensor(lamw[:, 0, :], lamv[:, 0, :], lamv[:, 1, :], ALU.mult), reads=[lamv], writes=[lamw])
            sc.op("dve", lambda e: e.tensor_tensor(lamw[:, 1, :], lamv[:, 2, :], lamv[:, 3, :], ALU.mult), reads=[lamv, lamw], writes=[lamw])
            sc.op("dve", lambda e: e.tensor_reduce(lams[:, 0:2], lamw[:], AX.X, ALU.add), reads=[lamw], writes=[lams])
            sc.op("act", lambda e: e.activation(lams[:, 0:2], lams[:, 0:2], AF.Exp), reads=[lams], writes=[lams])
            sc.op("dve", lambda e: e.tensor_tensor(lams[:, 2:3], lams[:, 0:1], lams[:, 1:2], ALU.subtract), reads=[lams], writes=[lams])
            sc.op("dve", lambda e: e.tensor_scalar(lams[:, 3:4], lams[:, 2:3], LAM_INIT, -1.0, ALU.add, ALU.mult), reads=[lams], writes=[lams])
            sc.barrier()
            sc.end_scope(e0)
        ident = cst[:, 0, :]
        ones = cst[:, 1, :]

        def mm_chain(pb, out_ap, pairs, reads, first_group=True):
            n = len(pairs)
            for i, (l, r) in enumerate(pairs):
                first = (i == 0)
                sc.op("pe", lambda e, l=l, r=r, i=i: e.matmul(out_ap, lhsT=l, rhs=r, start=(i == 0), stop=(i == n - 1)),
                      reads=reads, writes=[pb] if (first and first_group) else [],
                      chain=[] if (first and first_group) else [pb])

        def run_job(jb, has_sample):
            NTT = NT + (1 if has_sample else 0)
            T = NTT * 128
            tok0 = [t * 128 for t in range(NTT)]
            rb = jb * S
            G = lambda t0: (rb + t0) if t0 < S else (NPS * S + (t0 - S))
            with ExitStack() as e1:
              if not cfg.peer_only:
                  sb1 = lambda n, sh, dt=F32: sc.sbuf(n, sh, dt, e1)
                  xst = [sb1("xst0", [128, D])] * 2
                  wst = [sb1("wst0", [128, 8, 512])] * 2
                  wbf = [sb1("wbf0", [128, 8, 512], BF16)] * 2
                  bbc = [sb1(f"bbc{i}", [128, 512]) for i in range(2)]
                  ost = [sb1(f"ost{i}", [128, 512]) for i in range(3)]
                  QT = sb1("QT", [128, 8, T], BF16)
                  KT = sb1("KT", [128, 8, T], BF16)
                  Vx = sb1("Vx", [128, NT, 8, 129], BF16)
                  bq8 = sb1("bq8", [128, 8])
                  PT = [sb1(f"PT{i}", [128, 512], BF16) for i in range(4)]
                  oat = [sb1("oat0", [128, 8, 128])] * 2
                  rz = sb1("rz", [128, 2])
                  bqk = smallc[:, 56:72]
                  cmask = cst[:, 2, :]
                  sc.op("dve", lambda e: e.tensor_scalar(bq8[:], bqk[:, 0:8], 0.125, None, ALU.mult), reads=[smallc], writes=[bq8])
                  sc.op("pool", lambda e: e.memset(Vx[:], 1.0), writes=[Vx])
                  for ti in range(NTT):
                      t0 = tok0[ti]
                      xs = xst[ti % 2]
                      sc.dma("sp", xs[:, :], xin[G(t0):G(t0) + 128, :], [xin], [xs], xs)
                      for half in range(2):
                          pb = nps()
                          for j in range(4):
                              kc = half * 4 + j
                              sc.op("pe", lambda e, pb=pb, j=j, kc=kc, xs=xs: e.transpose(
                                  pb[:, j * 128:(j + 1) * 128], xs[:, kc * 128:(kc + 1) * 128], ident),
                                  reads=[xs, cst], writes=[pb] if j == 0 else [], chain=[] if j == 0 else [pb])
                          src = pb[:, :].rearrange("p (j r) -> p j r", j=4)
                          dst = xT[:, half * 4:half * 4 + 4, t0:t0 + 128]
                          if half == 0:
                              sc.op("act", lambda e, dst=dst, src=src: e.copy(dst, src), reads=[pb], writes=[xT])
                          else:
                              sc.op("dve", lambda e, dst=dst, src=src: e.tensor_copy(dst, src), reads=[pb], writes=[xT])
                  bi = 0

                  def load_blk(cc):
                      w_s, w_b = wst[bi % 2], wbf[bi % 2]
                      sc.dma("sp", w_s[:], w_in[:, cc:cc + 512].rearrange("(kc p) c -> p kc c", p=128), [w_in], [w_s], w_s)
                      sc.op("pool", lambda e, w_b=w_b, w_s=w_s: e.tensor_copy(w_b[:], w_s[:]), reads=[w_s], writes=[w_b])
                      return w_b
                  groups = [(g0, min(512, T - g0)) for g0 in range(0, T, 512)]
                  for blk in range(4 if cfg.a_qk else 0):
                      isq = blk < 2
                      w_b = load_blk(blk * 512)
                      bi += 1
                      for hh_ in range(4):
                          h = (blk % 2) * 4 + hh_
                          dstT = QT if isq else KT
                          for (g0, gn) in groups:
                              pb = nps()
                              mm_chain(pb, pb[:, 0:gn], [(w_b[:, kc, hh_ * 128:(hh_ + 1) * 128], xT[:, kc, g0:g0 + gn]) for kc in range(8)],
                                       [xT, w_b])
                              if isq:
                                  sc.op("act", lambda e, pb=pb, h=h, g0=g0, gn=gn: e.activation(
                                      QT[:, h, g0:g0 + gn], pb[:, 0:gn], AF.Identity, bias=bq8[:, h:h + 1], scale=0.125),
                                      reads=[pb, bq8], writes=[QT])
                              else:
                                  sc.op("act", lambda e, pb=pb, h=h, g0=g0, gn=gn: e.activation(
                                      KT[:, h, g0:g0 + gn], pb[:, 0:gn], AF.Identity, bias=bqk[:, 8 + h:9 + h]),
                                      reads=[pb, smallc], writes=[KT])
                  tm_blocks = [("k", 1024, 1024, k_out), ("v", 2048, 1024, v_out), ("u", 3072, 1024, conv_out)]
                  for name, c0, ncols, odram in tm_blocks:
                      for cb in range(0, ncols, 512):
                          b_b = bbc[bi % 2]
                          cc = c0 + cb
                          w_b = load_blk(cc)
                          sc.dma("sp", b_b[:], b_in[0:1, cc:cc + 512].partition_broadcast(128), [b_in], [b_b], b_b)
                          for ti in range(NTT):
                              t0 = tok0[ti]
                              pb = nps()
                              mm_chain(pb, pb[:, :], [(xT[:, kc, t0:t0 + 128], w_b[:, kc, :]) for kc in range(8)], [xT, w_b])
                              o = ost[(bi * NTT + ti) % 3]
                              sc.op("dve", lambda e, o=o, pb=pb, b_b=b_b: e.tensor_tensor(
                                  o[:, :], pb[:, :], b_b[:, :], ALU.add), reads=[pb, b_b], writes=[o])
                              if name == "v" and ti < NT and cfg.a_vx:
                                  hb0 = cb // 128
                                  sc.op("act", lambda e, o=o, ti=ti, hb0=hb0: e.copy(
                                      Vx[:, ti, hb0:hb0 + 4, 0:128], o[:, :].rearrange("p (h e) -> p h e", h=4)),
                                      reads=[o], writes=[Vx])
                              sc.dma("sp", odram[G(t0):G(t0) + 128, cb:cb + 512], o[:, :], [o], [odram], o, is_output=True)
                          bi += 1
                  psS = ps[0:4]
                  psO = ps[4:8]
                  si = 0
                  pti = 0
                  for qi in range(NT if cfg.attn else 0):
                      q0 = tok0[qi]
                      ot = oat[qi % 2]
                      for h in range(8):
                          acc = [psO[(h % 2) * 2 + c] for c in range(2)]
                          for kg in range(0, qi + 1, 4):
                              kis = list(range(kg, min(kg + 4, qi + 1)))
                              n = len(kis)
                              bankc = [psS[si % 4], psS[(si + 1) % 4]]; si += 2
                              ptc = [PT[pti % 4], PT[(pti + 1) % 4]]; pti += 2
                              for c in range(2):
                                  for idx, ki in enumerate(kis):
                                      k0 = tok0[ki]
                                      mm_chain(bankc[c], bankc[c][:, idx * 128:(idx + 1) * 128],
                                               [(KT[c * 64:(c + 1) * 64, h, k0:k0 + 128], QT[c * 64:(c + 1) * 64, h, q0:q0 + 128])],
                                               [KT, QT], first_group=(idx == 0))
                              for c in range(2):
                                  sc.op("act", lambda e, c=c, ptc=ptc, bankc=bankc, n=n: e.activation(
                                      ptc[c][:, 0:n * 128], bankc[c][:, 0:n * 128], AF.Exp), reads=[bankc[c]], writes=[ptc[c]])
                              if qi in kis:
                                  idx = qi - kg
                                  for c in range(2):
                                      sc.op("dve", lambda e, ptc=ptc, c=c, idx=idx: e.tensor_tensor(
                                          ptc[c][:, idx * 128:(idx + 1) * 128], ptc[c][:, idx * 128:(idx + 1) * 128], cmask, ALU.mult),
                                          reads=[ptc[c], cst], writes=[ptc[c]])
                              for idx, ki in enumerate(kis):
                                  for c in range(2):
                                      first = (ki == 0)
                                      sc.op("pe", lambda e, c=c, ptc=ptc, idx=idx, ki=ki, h=h, acc=acc, qi=qi: e.matmul(
                                          acc[c][:, 0:129], lhsT=ptc[c][:, idx * 128:(idx + 1) * 128], rhs=Vx[:, ki, h, :],
                                          start=(ki == 0), stop=(ki == qi)),
                                          reads=[ptc[c], Vx], writes=[acc[c]] if first else [], chain=[] if first else [acc[c]])
                          sc.op("dve", lambda e, acc=acc: e.reciprocal(rz[:, 0:1], acc[0][:, 128:129]), reads=[acc[0]], writes=[rz])
                          sc.op("dve", lambda e, acc=acc: e.reciprocal(rz[:, 1:2], acc[1][:, 128:129]), reads=[acc[1], rz], writes=[rz])
                          sc.op("dve", lambda e: e.tensor_tensor(rz[:, 1:2], rz[:, 1:2], lams[:, 3:4], ALU.mult), reads=[rz, lams], writes=[rz])
                          sc.op("act", lambda e, acc=acc, ot=ot, h=h: e.activation(ot[:, h, :], acc[0][:, 0:128], AF.Copy, scale=rz[:, 0:1]),
                                reads=[acc[0], rz], writes=[ot])
                          sc.op("dve", lambda e, acc=acc, ot=ot, h=h: e.scalar_tensor_tensor(
                              ot[:, h, :], acc[1][:, 0:128], rz[:, 1:2], ot[:, h, :], ALU.mult, ALU.add),
                              reads=[acc[1], rz, ot], writes=[ot])
                      sc.dma("sp", oa_sc[G(q0):G(q0) + 128, :], ot[:].rearrange("p h e -> p (h e)"), [ot], [oa_sc], ot)
                  if has_sample:
                      ot = oat[NT % 2]
                      sc.op("pool", lambda e, ot=ot: e.memset(ot[:], 0.0), writes=[ot])
                      sc.dma("sp", oa_sc[G(S):G(S) + 128, :], ot[:].rearrange("p h e -> p (h e)"), [ot], [oa_sc], ot)
                  sc.barrier()
                  sc.end_scope(e1)

            if has_sample and cfg.phaseS:
              with ExitStack() as e6:
                sb6 = lambda n, sh, dt=F32: sc.sbuf(n, sh, dt, e6)
                NPG = NSQ * NPP
                pti = sb6("pti", [128, NPG], I32); ptf = sb6("ptf", [128, NPG]); idx = sb6("idx", [128, NPG], I32)
                wqs = sb6("wqs", [128, 8, 128]); wq = sb6("wq", [128, 8, 1024], BF16)
                bq2 = sb6("bq2", [64, 16])
                Qs = [sb6(f"Qs{c}", [64, 8, 128]) for c in range(2)]
                kp = [sb6(f"kp{i}", [128, 1024]) for i in range(2)]
                vp = [sb6(f"vp{i}", [128, 1024]) for i in range(2)]
                kT = [sb6(f"kTs{i}", [64, 4, 128]) for i in range(2)]
                PTs = [sb6(f"PTs{i}", [128, 64]) for i in range(2)]
                knj = sb6("knj", [4, 1024]); vnj = sb6("vnj", [4, 1024])
                kTn = sb6("kTn", [64, 16, 4]); sn = sb6("sn", [4, 64]); PTn = sb6("PTn", [4, 64])
                zer = sb6("zer", [128, 512]); ones2 = sb6("ones2", [128, 2])
                rzs = sb6("rzs", [4, 32]); oaj = sb6("oaj", [4, 8, 128])
                psT = ps[0:2]; psSb = ps[2]; psOs = ps[3:7]; psZ = ps[7]
                sc.op("pool", lambda e: e.memset(zer[:], 0.0), writes=[zer])
                sc.op("pool", lambda e: e.memset(ones2[:], 1.0), writes=[ones2])
                sc.dma("sp", pti[:], ptab_d[0:1, :].partition_broadcast(128), [ptab_d], [pti], pti)
                sc.op("dve", lambda e: e.tensor_copy(ptf[:], pti[:]), reads=[pti], writes=[ptf])
                sc.op("dve", lambda e: e.tensor_scalar(ptf[:], ptf[:], 128.0, smallc[:, 88:89], ALU.mult, ALU.add), reads=[ptf, smallc], writes=[ptf])
                sc.op("dve", lambda e: e.tensor_copy(idx[:], ptf[:]), reads=[ptf], writes=[idx])
                for blk in range(8):
                    sc.dma("sp", wqs[:], w_in[:, blk * 128:(blk + 1) * 128].rearrange("(kc p) c -> p kc c", p=128), [w_in], [wqs], wqs)
                    sc.op("pool", lambda e, blk=blk: e.tensor_copy(wq[:, :, blk * 128:(blk + 1) * 128], wqs[:]), reads=[wqs], writes=[wq])
                sc.op("dve", lambda e: e.tensor_scalar(bq2[:], smallc[0:64, 72:88], 0.125, None, ALU.mult), reads=[smallc], writes=[bq2])
                for h in range(8):
                    for c in range(2):
                        pb = nps2()
                        mm_chain(pb, pb[0:64, 0:128], [(wq[:, kc, h * 128 + c * 64:h * 128 + c * 64 + 64], xT[:, kc, S:S + 128]) for kc in range(8)], [wq, xT])
                        sc.op("act", lambda e, pb=pb, h=h, c=c: e.activation(
                            Qs[c][:, h, :], pb[0:64, 0:128], AF.Identity, bias=bq2[:, h * 2 + c:h * 2 + c + 1], scale=0.125),
                            reads=[pb, bq2], writes=[Qs[c]])
                pgi = 0
                for j in range(NSQ):
                    qs = slice(4 * j, 4 * j + 4)
                    for bk in list(psOs) + [psZ]:
                        sc.op("pe", lambda e, bk=bk: e.matmul(bk[0:4, :], lhsT=zer[:, 0:4], rhs=zer[:, :], start=True, stop=True),
                              reads=[zer], writes=[bk])

                    def accum(ptile, nk, vtile):
                        for h in range(8):
                            for c in range(2):
                                hc = h * 2 + c
                                bk = psOs[hc // 4]
                                sc.op("pe", lambda e, bk=bk, hc=hc, h=h: e.matmul(
                                    bk[0:4, (hc % 4) * 128:(hc % 4 + 1) * 128], lhsT=ptile[0:nk, hc * 4:hc * 4 + 4], rhs=vtile[0:nk, h * 128:(h + 1) * 128],
                                    start=False, stop=False, skip_group_check=True), reads=[ptile, vtile], chain=[bk])
                                sc.op("pe", lambda e, hc=hc: e.matmul(
                                    psZ[0:4, hc * 2:hc * 2 + 2], lhsT=ptile[0:nk, hc * 4:hc * 4 + 4], rhs=ones2[0:nk, :],
                                    start=False, stop=False, skip_group_check=True), reads=[ptile, ones2], chain=[psZ])
                    for p in range(NPP):
                        kpt, vpt = kp[pgi % 2], vp[pgi % 2]
                        col = j * NPP + p
                        sc.dma("pool", kpt[:], ck_d[:, :], [ck_d, idx], [kpt], kpt,
                               indirect=dict(out_offset=None, in_offset=bass.IndirectOffsetOnAxis(ap=idx[:, col:col + 1], axis=0)))
                        sc.dma("pool", vpt[:], cv_d[:, :], [cv_d, idx], [vpt], vpt,
                               indirect=dict(out_offset=None, in_offset=bass.IndirectOffsetOnAxis(ap=idx[:, col:col + 1], axis=0)))
                        first_s = True
                        for hq in range(4):
                            pt_ = psT[hq % 2]
                            for hl in range(2):
                                for c in range(2):
                                    h = hq * 2 + hl
                                    o0 = (hl * 2 + c) * 128
                                    f0 = (hl == 0 and c == 0)
                                    sc.op("pe", lambda e, pt_=pt_, o0=o0, h=h, c=c, kpt=kpt: e.transpose(
                                        pt_[0:64, o0:o0 + 128], kpt[:, h * 128 + c * 64:h * 128 + c * 64 + 64], ident),
                                        reads=[kpt, cst], writes=[pt_] if f0 else [], chain=[] if f0 else [pt_])
                            kTt = kT[hq % 2]
                            if hq % 2 == 0:
                                sc.op("act", lambda e, pt_=pt_, kTt=kTt: e.copy(kTt[:].rearrange("p a k -> p (a k)"), pt_[0:64, :]), reads=[pt_], writes=[kTt])
                            else:
                                sc.op("dve", lambda e, pt_=pt_, kTt=kTt: e.tensor_copy(kTt[:].rearrange("p a k -> p (a k)"), pt_[0:64, :]), reads=[pt_], writes=[kTt])
                            for hl in range(2):
                                for c in range(2):
                                    h = hq * 2 + hl
                                    hc = h * 2 + c
                                    sc.op("pe", lambda e, kTt=kTt, hl=hl, c=c, h=h, hc=hc: e.matmul(
                                        psSb[:, hc * 4:hc * 4 + 4], lhsT=kTt[:, hl * 2 + c, :], rhs=Qs[c][:, h, qs], start=True, stop=True),
                                        reads=[kTt, Qs[c]], writes=[psSb] if first_s else [], chain=[] if first_s else [psSb])
                                    first_s = False
                        ptile = PTs[pgi % 2]
                        sc.op("act", lambda e, ptile=ptile: e.activation(ptile[:], psSb[:, 0:64], AF.Exp), reads=[psSb], writes=[ptile])
                        accum(ptile, 128, vpt)
                        pgi += 1
                    r0 = G(S) + 4 * j
                    sc.dma("sp", knj[:], k_out[r0:r0 + 4, :], [k_out], [knj], knj)
                    sc.dma("sp", vnj[:], v_out[r0:r0 + 4, :], [v_out], [vnj], vnj)
                    pt_ = psT[0]
                    for hc in range(16):
                        sc.op("pe", lambda e, hc=hc, pt_=pt_: e.transpose(pt_[0:64, hc * 4:hc * 4 + 4], knj[0:4, hc * 64:hc * 64 + 64], ident[0:4, 0:4]),
                              reads=[knj, cst], writes=[pt_] if hc == 0 else [], chain=[] if hc == 0 else [pt_])
                    sc.op("act", lambda e, pt_=pt_: e.copy(kTn[:].rearrange("p a k -> p (a k)"), pt_[0:64, 0:64]), reads=[pt_], writes=[kTn])
                    for hc in range(16):
                        h, c = hc // 2, hc % 2
                        sc.op("pe", lambda e, hc=hc, h=h, c=c: e.matmul(
                            psSb[0:4, hc * 4:hc * 4 + 4], lhsT=kTn[:, hc, :], rhs=Qs[c][:, h, qs], start=True, stop=True),
                            reads=[kTn, Qs[c]], writes=[psSb] if hc == 0 else [], chain=[] if hc == 0 else [psSb])
                    sc.op("dve", lambda e: e.tensor_tensor(sn[:], psSb[0:4, 0:64], cst[0:4, 26, 0:64], ALU.add), reads=[psSb, cst], writes=[sn])
                    sc.op("act", lambda e: e.activation(PTn[:], sn[:], AF.Exp), reads=[sn], writes=[PTn])
                    accum(PTn, 4, vnj)
                    sc.op("dve", lambda e: e.reciprocal(rzs[:], psZ[0:4, 0:32]), reads=[psZ], writes=[rzs])
                    for h in range(8):
                        c1 = 2 * (2 * h + 1)
                        sc.op("dve", lambda e, c1=c1: e.tensor_tensor(rzs[:, c1:c1 + 1], rzs[:, c1:c1 + 1], lams[0:4, 3:4], ALU.mult),
                              reads=[rzs, lams], writes=[rzs])
                    for h in range(8):
                        b0, o0 = psOs[(2 * h) // 4], ((2 * h) % 4) * 128
                        b1, o1 = psOs[(2 * h + 1) // 4], ((2 * h + 1) % 4) * 128
                        sc.op("act", lambda e, h=h, b0=b0, o0=o0: e.activation(oaj[:, h, :], b0[0:4, o0:o0 + 128], AF.Copy, scale=rzs[:, 4 * h:4 * h + 1]),
                              reads=[b0, rzs], writes=[oaj])
                        sc.op("dve", lambda e, h=h, b1=b1, o1=o1: e.scalar_tensor_tensor(
                            oaj[:, h, :], b1[0:4, o1:o1 + 128], rzs[:, 4 * h + 2:4 * h + 3], oaj[:, h, :], ALU.mult, ALU.add),
                            reads=[b1, rzs, oaj], writes=[oaj])
                    sc.dma("sp", oa_sc[r0:r0 + 4, :], oaj[:].rearrange("p h e -> p (h e)"), [oaj], [oa_sc], oaj)
                sc.barrier()
                sc.end_scope(e6)

            with ExitStack() as e2:
              if not cfg.peer_only:
                  sb = lambda n, sh, dt=F32: sc.sbuf(n, sh, dt, e2)
                  wu = sb("wu", [128, 8, 1024], BF16)
                  wvm = sb("wvm", [128, 8, 1024], BF16)
                  wg = sb("wg", [128, 8, 8], BF16)
                  wgs = sb("wgs", [128, 8, 8])
                  wqm = sb("wqm", [128, 4, 2, 256])
                  wkm = sb("wkm", [128, 4, 2, 256])
                  bvm = sb("bvm", [128, 1024], BF16)
                  bg = sb("bg", [128, 8])
                  mst = sb("mst_sb", [128, 4 + 4 * NSQ])
                  mstp = sb("mstp", [128, 4])
                  CT = [[sb(f"CT{j}_{h}", [128, 2, 257]) for h in range(4)] for j in range(1 + (NSQ if has_sample else 0))]
                  uTe = sb("uTe", [128, 8, 131])
                  uTsV = uTe[:, :, 0:NSQ * 7].rearrange("p c (s t) -> p c s t", t=7)
                  ucT = sb("ucT", [128, 8, 128])
                  qT = sb("qT", [128, 4, 2, 128])
                  kT = sb("kT", [128, 4, 2, 128])
                  ktm = sb("ktm", [128, 4, 256])
                  vext = sb("vext", [128, 4, 257])
                  g = sb("g", [128, 8])
                  t1 = sb("t1", [128, 4]); t2 = sb("t2", [128, 4])
                  bmt = sb("bmt", [128, 8])
                  imb = sb("imb", [128, 4])
                  Dm = sb("Dm", [128, 4, 128])
                  swT = sb("swT", [128, 4, 128])
                  diag = swT
                  rmax = sb("rmax", [128, 4]); bm = sb("bm", [128, 4]); negm = sb("negm", [128, 4])
                  inter = sb("inter", [128, 4]); eneg = sb("eneg", [128, 4]); dlt = sb("dlt", [128, 4])
                  Bs = sb("Bs", [128, 257]); nd = sb("nd", [128, 257])
                  den = sb("den", [128, 1]); rden = sb("rden", [128, 1])
                  hh = sb("hh", [128, 4, 256])
                  ctoV = Dm[:].rearrange("p (a b) s -> p a (b s)", a=2)
                  qTmV = Bs[:, 0:256].rearrange("p (d t) -> p d t", d=2)
                  wkV = Bs[:, 0:256]
                  lr = sb("lr", [128, 8]); lj = sb("lj", [128, 8])
                  wv = sb("wv", [128, 4]); wj = sb("wj", [128, 4]); tmp4 = sb("tmp4", [128, 4]); dec = sb("dec", [128, 4])

                  for (dstw, c00) in ((wu, 3072), (wvm, 4096)):
                      for blk in range(8):
                          wstV = hh[:].rearrange("p h (a b) -> p (h a) b", a=2)
                          sc.dma("sp", wstV, w_in[:, c00 + blk * 128:c00 + (blk + 1) * 128].rearrange("(kc p) c -> p kc c", p=128),
                                 [w_in], [hh], hh)
                          sc.op("pool", lambda e, blk=blk, dstw=dstw, wstV=wstV: e.tensor_copy(dstw[:, :, blk * 128:(blk + 1) * 128], wstV),
                                reads=[hh], writes=[dstw])
                  sc.dma("sp", wgs[:], w_in[:, 6144:6152].rearrange("(kc p) c -> p kc c", p=128), [w_in], [wgs], wgs)
                  sc.op("pool", lambda e: e.tensor_copy(wg[:], wgs[:]), reads=[wgs], writes=[wg])
                  sc.dma("sp", wqm[:], wqm_d[:].rearrange("h (dc p) e -> p h dc e", p=128), [wqm_d], [wqm], wqm)
                  sc.dma("sp", wkm[:], wkm_d[:].rearrange("h (dc p) e -> p h dc e", p=128), [wkm_d], [wkm], wkm)
                  sc.dma("sp", ucT[:].rearrange("p c t -> p (c t)"), b_in[0:1, 4096:5120].partition_broadcast(128), [b_in], [ucT], ucT)
                  sc.op("pool", lambda e: e.tensor_copy(bvm[:], ucT[:].rearrange("p c t -> p (c t)")), reads=[ucT], writes=[bvm])
                  sc.dma("sp", bg[:], b_in[0:1, 6144:6152].partition_broadcast(128), [b_in], [bg], bg)
                  sc.dma("sp", mst[:], mst_d[:], [mst_d], [mst], mst)
                  buT = smallc[:, 0:8]
                  wcT = smallc[:, 8:40].rearrange("p (c j) -> p c j", j=4)
                  bcT = smallc[:, 40:48]
                  rowmask = smallc[:, 48:56]
                  for h in range(4):
                      sc.op("pool", lambda e, h=h: e.memset(CT[0][h][:], 0.0), writes=[CT[0][h]])
                  sc.op("pool", lambda e: e.memset(uTe[:], 0.0), writes=[uTe])
                  sc.op("pool", lambda e: e.memset(mstp[:], 0.0), writes=[mstp])
                  sc.op("pool", lambda e: e.memset(ucT[:], 0.0), writes=[ucT])
                  sc.op("pool", lambda e: e.memset(vext[:], 1.0), writes=[vext])
                  for j in range(NSQ if has_sample else 0):
                      for h in range(4):
                          sc.dma("sp", ctoV, stC_d[j, h].rearrange("(ec p) d -> p ec d", p=128), [stC_d], [Dm], Dm)
                          pb = nps()
                          for ec in range(2):
                              for dc in range(2):
                                  first = (ec == 0 and dc == 0)
                                  sc.op("pe", lambda e, pb=pb, ec=ec, dc=dc: e.transpose(
                                      pb[:, (dc * 2 + ec) * 128:(dc * 2 + ec + 1) * 128], ctoV[:, ec, dc * 128:(dc + 1) * 128], ident),
                                      reads=[Dm, cst], writes=[pb] if first else [], chain=[] if first else [pb])
                          sc.op("act", lambda e, pb=pb, j=j, h=h: e.copy(
                              CT[j + 1][h][:, :, 0:256], pb[:, :].rearrange("p (dc e) -> p dc e", dc=2)),
                              reads=[pb], writes=[CT[j + 1][h]])
                          sc.dma("sp", CT[j + 1][h][:, :, 256:257], stn_d[j, h].rearrange("(dc p o) -> p dc o", p=128, o=1),
                                 [stn_d], [CT[j + 1][h]], CT[j + 1][h], slow=True)

                  for ti in range(NTT):
                      t0 = tok0[ti]
                      sample = (ti == NT)
                      if sample:
                          for c_ in range(8):
                              sc.dma("sp", uTsV[:, c_, :, 0:3], convT_d[:, c_], [convT_d], [uTe], uTe, slow=True)
                      xs_tok = [xT[:, kc, t0:t0 + 128] for kc in range(8)]
                      for half in range(2):
                          pb = nps()
                          for c4 in range(4):
                              c = half * 4 + c4
                              mm_chain(pb, pb[:, c4 * 128:(c4 + 1) * 128],
                                       [(wu[:, kc, c * 128:(c + 1) * 128], xs_tok[kc]) for kc in range(8)], [wu, xT],
                                       first_group=(c4 == 0))
                          for c4 in range(4):
                              c = half * 4 + c4
                              if not sample:
                                  sc.op("act", lambda e, pb=pb, c=c, c4=c4: e.activation(
                                      uTe[:, c, 3:131], pb[:, c4 * 128:(c4 + 1) * 128], AF.Identity, bias=buT[:, c:c + 1]),
                                      reads=[pb, smallc], writes=[uTe])
                              else:
                                  sc.op("act", lambda e, pb=pb, c=c, c4=c4: e.activation(
                                      uTsV[:, c, :, 3:7], pb[:, c4 * 128:c4 * 128 + 4 * NSQ].rearrange("p (s t) -> p s t", t=4),
                                      AF.Identity, bias=buT[:, c:c + 1]),
                                      reads=[pb, smallc], writes=[uTe])
                      for c in range(8):
                          if not sample:
                              src = lambda j, c=c: uTe[:, c, j:j + 128]
                              dst = ucT[:, c, :]
                              rd = [uTe, smallc]
                          else:
                              src = lambda j, c=c: uTsV[:, c, :, j:j + 4]
                              dst = ucT[:, c, 0:4 * NSQ].rearrange("p (s t) -> p s t", t=4)
                              rd = [uTe, smallc]
                          sc.op("dve", lambda e, src=src, dst=dst, c=c: e.tensor_scalar(
                              dst, src(0), wcT[:, c, 0:1], bcT[:, c:c + 1], ALU.mult, ALU.add), reads=rd, writes=[ucT])
                          for j in range(1, 4):
                              sc.op("dve", lambda e, src=src, dst=dst, c=c, j=j: e.scalar_tensor_tensor(
                                  dst, src(j), wcT[:, c, j:j + 1], dst, ALU.mult, ALU.add), reads=rd + [ucT], writes=[ucT])
                      sc.op("act", lambda e: e.activation(ucT[:], ucT[:], AF.Silu), reads=[ucT], writes=[ucT])
                      if not sample:
                          sc.op("dve", lambda e: e.tensor_copy(uTe[:, :, 0:3], uTe[:, :, 128:131]), reads=[uTe], writes=[uTe])
                      for (wsrc, dstT, scale) in ((wqm, qT, 1.0), (wkm, kT, 1.0 / 16.0)):
                          for hp in range(2):
                              pb = nps()
                              for hh_ in range(2):
                                  h = hp * 2 + hh_
                                  for ec in range(2):
                                      o0 = (hh_ * 2 + ec) * 128
                                      mm_chain(pb, pb[:, o0:o0 + 128],
                                               [(wsrc[:, h, dc, ec * 128:(ec + 1) * 128], ucT[:, h * 2 + dc, :]) for dc in range(2)],
                                               [wsrc, ucT], first_group=(hh_ == 0 and ec == 0))
                              sc.op("act", lambda e, pb=pb, hp=hp, dstT=dstT, scale=scale: e.activation(
                                  dstT[:, hp * 2:hp * 2 + 2, :, :], pb[:, :].rearrange("p (h ec t) -> p h ec t", h=2, ec=2),
                                  AF.Copy, scale=scale), reads=[pb], writes=[dstT])
                      for hp in range(2):
                          pb = nps()
                          for hh_ in range(2):
                              h = hp * 2 + hh_
                              mm_chain(pb, pb[:, hh_ * 256:(hh_ + 1) * 256],
                                       [(ucT[:, h * 2 + dc, :], wkm[:, h, dc, :]) for dc in range(2)], [wkm, ucT],
                                       first_group=(hh_ == 0))
                          sc.op("act", lambda e, pb=pb, hp=hp: e.activation(
                              ktm[:, hp * 2:hp * 2 + 2, :], pb[:, :].rearrange("p (h e) -> p h e", h=2), AF.Copy, scale=1.0 / 16.0),
                              reads=[pb], writes=[ktm])
                      for blk in range(2):
                          pb = nps()
                          mm_chain(pb, pb[:, :], [(xs_tok[kc], wvm[:, kc, blk * 512:(blk + 1) * 512]) for kc in range(8)], [xT, wvm])
                          sc.op("dve", lambda e, pb=pb, blk=blk: e.tensor_tensor(
                              vext[:, blk * 2:blk * 2 + 2, 0:256], pb[:, :].rearrange("p (h e) -> p h e", h=2),
                              bvm[:, blk * 512:(blk + 1) * 512].rearrange("p (h e) -> p h e", h=2), ALU.add),
                              reads=[pb, bvm], writes=[vext])
                      pb = nps()
                      mm_chain(pb, pb[:, 0:8], [(xs_tok[kc], wg[:, kc, :]) for kc in range(8)], [xT, wg])
                      sc.op("dve", lambda e, pb=pb: e.tensor_tensor(g[:], pb[:, 0:8], bg[:], ALU.add), reads=[pb, bg], writes=[g])
                      tri = cst[:, 3 if sample else 2, :]
                      mneg = cst[:, 5 if sample else 4, :]
                      selrow = cst[:, 15 if sample else 6, :]
                      mstb = mst if sample else mstp
                      mst_cur = mstb[:, 0:4]
                      sc.op("act", lambda e: e.activation(t1[:], g[:, 4:8], AF.Exp, scale=-1.0), reads=[g], writes=[t1])
                      sc.op("act", lambda e: e.activation(t2[:], t1[:], AF.Ln, bias=1.0), reads=[t1], writes=[t2])
                      pb = nps()
                      mm_chain(pb, pb[:, 0:4], [(tri, t2[:])], [cst, t2])
                      sc.op("dve", lambda e, pb=pb: e.tensor_scalar(bmt[:, 0:4], pb[:, 0:4], -1.0, None, ALU.mult),
                            reads=[pb], writes=[bmt])
                      sc.op("dve", lambda e: e.tensor_tensor(imb[:], g[:, 0:4], bmt[:, 0:4], ALU.subtract), reads=[g, bmt], writes=[imb])
                      for h in range(4):
                          sc.op("dve", lambda e, h=h: e.tensor_scalar(diag[:, h, :], ident, imb[:, h:h + 1], None, ALU.mult),
                                reads=[cst, imb], writes=[diag])
                      pb = nps()
                      mm_chain(pb, pb[:, :], [(ones, diag[:].rearrange("p h s -> p (h s)"))], [cst, diag])
                      for h in range(4):
                          sc.op("dve", lambda e, pb=pb, h=h: e.scalar_tensor_tensor(
                              Dm[:, h, :], pb[:, h * 128:(h + 1) * 128], bmt[:, h:h + 1], mneg, ALU.add, ALU.add),
                              reads=[pb, bmt, cst], writes=[Dm])
                      sc.op("dve", lambda e: e.tensor_reduce(rmax[:], Dm[:], AX.X, ALU.max), reads=[Dm], writes=[rmax])
                      sc.op("dve", lambda e: e.tensor_tensor(bm[:], bmt[:, 0:4], mst_cur, ALU.add), reads=[bmt, mstb], writes=[bm])
                      sc.op("dve", lambda e: e.tensor_tensor(bmt[:, 4:8], bm[:], rmax[:], ALU.max), reads=[bm, rmax, bmt], writes=[bmt])
                      sc.op("dve", lambda e: e.tensor_scalar(negm[:], bmt[:, 4:8], -1.0, None, ALU.mult), reads=[bmt], writes=[negm])
                      for h in range(4):
                          sc.op("act", lambda e, h=h: e.activation(Dm[:, h, :], Dm[:, h, :], AF.Exp, bias=negm[:, h:h + 1]),
                                reads=[Dm, negm], writes=[Dm])
                      sc.op("dve", lambda e: e.tensor_tensor(dlt[:], bm[:], bmt[:, 4:8], ALU.subtract), reads=[bm, bmt], writes=[dlt])
                      sc.op("act", lambda e: e.activation(inter[:], dlt[:], AF.Exp), reads=[dlt], writes=[inter])
                      sc.op("act", lambda e: e.activation(eneg[:], bmt[:, 4:8], AF.Exp, scale=-1.0), reads=[bmt], writes=[eneg])
                      pb = nps()
                      for h in range(4):
                          mm_chain(pb, pb[:, h * 128:(h + 1) * 128], [(qT[:, h, dc, :], kT[:, h, dc, :]) for dc in range(2)],
                                   [qT, kT], first_group=(h == 0))
                      sc.op("dve", lambda e, pb=pb: e.tensor_tensor(Dm[:].rearrange("p h s -> p (h s)"), pb[:, :],
                                                                   Dm[:].rearrange("p h s -> p (h s)"), ALU.mult),
                            reads=[pb, Dm], writes=[Dm])
                      pb = nps()
                      for h in range(4):
                          sc.op("pe", lambda e, pb=pb, h=h: e.transpose(pb[:, h * 128:(h + 1) * 128], Dm[:, h, :], ident),
                                reads=[Dm, cst], writes=[pb] if h == 0 else [], chain=[] if h == 0 else [pb])
                      sc.op("act", lambda e, pb=pb: e.copy(swT[:].rearrange("p h s -> p (h s)"), pb[:, :]), reads=[pb], writes=[swT])
                      seqs = [0] if not sample else list(range(1, NSQ + 1))
                      for h in range(4):
                          pa = nps()
                          pairs = []
                          if not sample:
                              pairs = [(qT[:, h, dc, :], CT[0][h][:, dc, :]) for dc in range(2)]
                              rds = [qT, CT[0][h]]
                          else:
                              nmm = 2 * len(seqs)
                              im = 0
                              for j in seqs:
                                  for dc in range(2):
                                      sc.op("dve", lambda e, h=h, dc=dc, j=j: e.tensor_tensor(
                                          qTmV[:, dc, :], qT[:, h, dc, :], cst[:, 15 + j, :], ALU.mult),
                                          reads=[qT, cst], writes=[Bs])
                                  for dc in range(2):
                                      sc.op("pe", lambda e, pa=pa, dc=dc, j=j, h=h, im=im: e.matmul(
                                          pa[:, 0:257], lhsT=qTmV[:, dc, :], rhs=CT[j][h][:, dc, :],
                                          start=(im == 0), stop=(im == nmm - 1)),
                                          reads=[Bs, CT[j][h]], writes=[pa] if im == 0 else [], chain=[] if im == 0 else [pa])
                                      im += 1
                          if not sample:
                              mm_chain(pa, pa[:, 0:257], pairs, rds)
                          pb2 = nps()
                          mm_chain(pb2, pb2[:, 0:257], [(swT[:, h, :], vext[:, h, :])], [swT, vext])
                          sc.op("act", lambda e, pb2=pb2: e.copy(Bs[:], pb2[:, 0:257]), reads=[pb2], writes=[Bs])
                          sc.op("dve", lambda e, pa=pa, h=h: e.scalar_tensor_tensor(
                              nd[:], pa[:, 0:257], inter[:, h:h + 1], Bs[:], ALU.mult, ALU.add), reads=[pa, inter, Bs], writes=[nd])
                          sc.op("dve", lambda e: e.tensor_scalar(rden[:], nd[:, 256:257], -1.0, None, ALU.mult), reads=[nd], writes=[rden])
                          sc.op("dve", lambda e: e.tensor_tensor(den[:], nd[:, 256:257], rden[:], ALU.max), reads=[nd, rden], writes=[den])
                          sc.op("dve", lambda e, h=h: e.tensor_tensor(den[:], den[:], eneg[:, h:h + 1], ALU.max), reads=[den, eneg], writes=[den])
                          sc.op("dve", lambda e: e.reciprocal(rden[:], den[:]), reads=[den], writes=[rden])
                          sc.op("dve", lambda e, h=h: e.tensor_scalar(hh[:, h, :], nd[:, 0:256], rden[:, 0:1], None, ALU.mult),
                                reads=[nd, rden], writes=[hh])
                      sc.dma("sp", h_out[G(t0):G(t0) + 128, :], hh[:].rearrange("p h e -> p (h e)"), [hh], [h_out], hh, is_output=True)
                      pb = nps()
                      mm_chain(pb, pb[:, 0:8], [(selrow, bmt[:])], [cst, bmt])
                      sc.op("act", lambda e, pb=pb: e.copy(lr[:], pb[:, 0:8]), reads=[pb], writes=[lr])
                      sc.op("dve", lambda e: e.tensor_tensor(tmp4[:], lr[:, 0:4], lr[:, 4:8], ALU.subtract), reads=[lr], writes=[tmp4])
                      sc.op("dve", lambda e: e.tensor_tensor(tmp4[:], tmp4[:], imb[:], ALU.add), reads=[tmp4, imb], writes=[tmp4])
                      sc.op("act", lambda e: e.activation(wv[:], tmp4[:], AF.Exp), reads=[tmp4], writes=[wv])
                      for j in seqs:
                          if not sample:
                              ljt = lr
                              mrep = mstp[:, 0:4]
                          else:
                              pbj = nps()
                              mm_chain(pbj, pbj[:, 0:8], [(cst[:, 6 + j, :], bmt[:])], [cst, bmt])
                              sc.op("act", lambda e, pbj=pbj: e.copy(lj[:], pbj[:, 0:8]), reads=[pbj], writes=[lj])
                              ljt = lj
                              mrep = mst[:, 4 * j:4 * j + 4]
                          sc.op("dve", lambda e, ljt=ljt, mrep=mrep: e.tensor_tensor(dec[:], ljt[:, 0:4], mrep, ALU.add),
                                reads=[ljt, mstb], writes=[dec])
                          sc.op("dve", lambda e, ljt=ljt: e.tensor_tensor(dec[:], dec[:], ljt[:, 4:8], ALU.subtract),
                                reads=[ljt, dec], writes=[dec])
                          sc.op("act", lambda e: e.activation(dec[:], dec[:], AF.Exp), reads=[dec], writes=[dec])
                          if not sample:
                              wjt = wv
                          else:
                              sc.op("dve", lambda e, j=j: e.tensor_scalar(wj[:], wv[:], rowmask[:, j - 1:j], None, ALU.mult),
                                    reads=[wv, smallc], writes=[wj])
                              wjt = wj
                          for h in range(4):
                              sc.op("dve", lambda e, h=h, wjt=wjt: e.tensor_scalar(wkV, ktm[:, h, :], wjt[:, h:h + 1], None, ALU.mult),
                                    reads=[ktm, wjt], writes=[Bs])
                              for dc in range(2):
                                  pu = nps()
                                  mm_chain(pu, pu[:, 0:257], [(wkV[:, dc * 128:(dc + 1) * 128], vext[:, h, :])], [Bs, vext])
                                  sc.op("dve", lambda e, pu=pu, j=j, h=h, dc=dc: e.scalar_tensor_tensor(
                                      CT[j][h][:, dc, :], CT[j][h][:, dc, :], dec[:, h:h + 1], pu[:, 0:257], ALU.mult, ALU.add),
                                      reads=[pu, dec, CT[j][h]], writes=[CT[j][h]])
                          sc.op("dve", lambda e, ljt=ljt, mrep=mrep: e.tensor_copy(mrep, ljt[:, 4:8]), reads=[ljt], writes=[mstb])

                  for j in range(1 + (NSQ if has_sample else 0)):
                      oj = jb if j == 0 else NPS + j - 1
                      for h in range(4):
                          pb = nps()
                          for dc in range(2):
                              for ec in range(2):
                                  first = (ec == 0 and dc == 0)
                                  sc.op("pe", lambda e, pb=pb, ec=ec, dc=dc, j=j, h=h: e.transpose(
                                      pb[:, (ec * 2 + dc) * 128:(ec * 2 + dc + 1) * 128], CT[j][h][:, dc, ec * 128:(ec + 1) * 128], ident),
                                      reads=[CT[j][h], cst], writes=[pb] if first else [], chain=[] if first else [pb])
                          sc.op("act", lambda e, pb=pb: e.copy(ctoV, pb[:, :].rearrange("p (ec d) -> p ec d", ec=2)),
                                reads=[pb], writes=[Dm])
                          sc.dma("sp", C_out[oj, h].rearrange("(ec p) d -> p ec d", p=128), ctoV, [Dm], [C_out], Dm, is_output=True)
                          sc.dma("sp", n_out[oj, h].rearrange("(dc p o) -> p dc o", p=128, o=1), CT[j][h][:, :, 256:257],
                                 [CT[j][h]], [n_out], CT[j][h], is_output=True, slow=True)
                      msrc = mstp[0:1, 0:4] if j == 0 else mst[0:1, 4 * j:4 * j + 4]
                      mb_ = mstp if j == 0 else mst
                      sc.dma("sp", m_out[oj:oj + 1, :], msrc, [mb_], [m_out], mb_, is_output=True)
                  sc.barrier()
                  sc.end_scope(e2)
            def load_w2(stg, dst, dram, c0, ncols=1024):
                for blk in range(ncols // 512):
                    sc.dma("sp", stg[:], dram[:, c0 + blk * 512:c0 + (blk + 1) * 512].rearrange("(kc p) c -> p kc c", p=128),
                           [dram], [stg], stg)
                    sc.op("pool", lambda e, blk=blk: e.tensor_copy(dst[:, :, blk * 512:(blk + 1) * 512], stg[:]),
                          reads=[stg], writes=[dst])

            def transpose_tm(src_tile, dstT, rd):
                for half in range(2):
                    pb = nps()
                    for j in range(4):
                        kc = half * 4 + j
                        sc.op("pe", lambda e, pb=pb, j=j, kc=kc: e.transpose(pb[:, j * 128:(j + 1) * 128], src_tile(kc), ident),
                              reads=[rd, cst], writes=[pb] if j == 0 else [], chain=[] if j == 0 else [pb])
                    src = pb[:, :].rearrange("p (j r) -> p j r", j=4)
                    if half == 0:
                        sc.op("act", lambda e, src=src: e.copy(dstT[:, 0:4, :], src), reads=[pb], writes=[dstT])
                    else:
                        sc.op("dve", lambda e, src=src: e.tensor_copy(dstT[:, 4:8, :], src), reads=[pb], writes=[dstT])

            if cfg.phaseC and not cfg.peer_only:
              with ExitStack() as e3:
                sb3 = lambda n, sh, dt=F32: sc.sbuf(n, sh, dt, e3)
                wstC = sb3("wstC", [128, 8, 512])
                w_ga = sb3("w_ga", [128, 8, 1024], BF16)
                w_a_sb = sb3("w_a_sb", [128, 8, 1024], BF16)
                b_ga = sb3("b_ga", [128, 1024])
                sublg = sb3("sublg", [128, 128])
                oat3 = sb3("oat3", [128, 8, 128]); sq = sb3("sq", [128, 8, 128])
                ms = sb3("ms", [128, 8]); rstd = sb3("rstd", [128, 8])
                oan = sb3("oan", [128, 8, 128]); oanT = sb3("oanT", [128, 8, 128], BF16)
                sg = sb3("sg", [128, 512]); mat = sb3("mat", [128, 1024])
                load_w2(wstC, w_ga, w_in, 6152)
                load_w2(wstC, w_a_sb, wa_d, 0)
                sc.dma("sp", b_ga[:], b_in[0:1, 6152:7176].partition_broadcast(128), [b_in], [b_ga], b_ga)
                sc.dma("sp", sublg[:], subg_d[0:1, :].partition_broadcast(128), [subg_d], [sublg], sublg)
                sc.op("dve", lambda e: e.tensor_scalar(sublg[:], sublg[:], 1.0 - LAM_INIT, None, ALU.mult), reads=[sublg], writes=[sublg])
                for ti in range(NTT):
                    t0 = tok0[ti]
                    sc.dma("sp", oat3[:].rearrange("p h e -> p (h e)"), oa_sc[G(t0):G(t0) + 128, :], [oa_sc], [oat3], oat3)
                    sc.op("dve", lambda e: e.tensor_tensor(sq[:], oat3[:], oat3[:], ALU.mult), reads=[oat3], writes=[sq])
                    sc.op("dve", lambda e: e.tensor_reduce(ms[:], sq[:], AX.X, ALU.add), reads=[sq], writes=[ms])
                    sc.op("act", lambda e: e.activation(ms[:], ms[:], AF.Sqrt, bias=epsc[:, 0:1], scale=1.0 / 128.0), reads=[ms, epsb], writes=[ms])
                    sc.op("dve", lambda e: e.reciprocal(rstd[:], ms[:]), reads=[ms], writes=[rstd])
                    for h in range(8):
                        sc.op("dve", lambda e, h=h: e.scalar_tensor_tensor(
                            oan[:, h, :], oat3[:, h, :], rstd[:, h:h + 1], sublg[:], ALU.mult, ALU.mult),
                            reads=[oat3, rstd, sublg], writes=[oan])
                    transpose_tm(lambda kc: oan[:, kc, :], oanT, oan)
                    for blk in range(2):
                        cs = slice(blk * 512, (blk + 1) * 512)
                        pga = nps()
                        mm_chain(pga, pga[:, :], [(xT[:, kc, t0:t0 + 128], w_ga[:, kc, cs]) for kc in range(8)], [xT, w_ga])
                        sc.op("dve", lambda e, pga=pga, cs=cs: e.tensor_tensor(sg[:], pga[:, :], b_ga[:, cs], ALU.add),
                              reads=[pga, b_ga], writes=[sg])
                        sc.op("act", lambda e: e.activation(sg[:], sg[:], AF.Sigmoid), reads=[sg], writes=[sg])
                        pya = nps()
                        mm_chain(pya, pya[:, :], [(oanT[:, ec, :], w_a_sb[:, ec, cs]) for ec in range(8)], [oanT, w_a_sb])
                        sc.op("dve", lambda e, pya=pya, cs=cs: e.tensor_tensor(mat[:, cs], pya[:, :], sg[:], ALU.mult),
                              reads=[pya, sg], writes=[mat])
                    sc.dma("sp", ma_sc[G(t0):G(t0) + 128, :], mat[:], [mat], [ma_sc], mat)
                sc.barrier()
                sc.end_scope(e3)

              with ExitStack() as e4:
                sb4 = lambda n, sh, dt=F32: sc.sbuf(n, sh, dt, e4)
                wstC = sb4("wstC2", [128, 8, 512])
                w_om = sb4("w_om", [128, 8, 1024], BF16)
                w_gb = sb4("w_gb", [128, 8, 1024], BF16)
                w_b_sb = sb4("w_b_sb", [128, 8, 1024], BF16)
                w_o_sb = sb4("w_o_sb", [128, 8, 1024], BF16)
                b_om = sb4("b_om", [128, 1024]); b_gb = sb4("b_gb", [128, 1024])
                mng = sb4("mng", [128, 1024]); l1g = sb4("l1g", [128, 1024]); l1b = sb4("l1b", [128, 1024])
                hht = sb4("hht", [128, 1024]); xt = sb4("xt", [128, 1024]); mat = sb4("mat2", [128, 1024])
                hn = sb4("hn", [128, 1024]); so = sb4("so", [128, 512]); mg = sb4("mg", [128, 1024])
                hbT = sb4("hbT", [128, 8, 128], BF16); mT = sb4("mT", [128, 8, 128], BF16)
                st = sb4("st", [128, 4, 6]); mv = sb4("mv", [128, 4, 2]); rs4 = sb4("rs4", [128, 4])
                st2 = sb4("st2", [128, 2, 6]); mv2 = sb4("mv2", [128, 2]); rs1 = sb4("rs1", [128, 1])
                load_w2(wstC, w_om, w_in, 5120)
                load_w2(wstC, w_gb, w_in, 7176)
                load_w2(wstC, w_b_sb, wb_d, 0)
                load_w2(wstC, w_o_sb, wo_d, 0)
                sc.dma("sp", b_om[:], b_in[0:1, 5120:6144].partition_broadcast(128), [b_in], [b_om], b_om)
                sc.dma("sp", b_gb[:], b_in[0:1, 7176:8200].partition_broadcast(128), [b_in], [b_gb], b_gb)
                sc.dma("sp", mng[:], vec_d[0:1, :].partition_broadcast(128), [vec_d], [mng], mng)
                sc.dma("sp", l1g[:], vec_d[1:2, :].partition_broadcast(128), [vec_d], [l1g], l1g)
                sc.dma("sp", l1b[:], vec_d[2:3, :].partition_broadcast(128), [vec_d], [l1b], l1b)
                for ti in range(NTT):
                    t0 = tok0[ti]
                    sc.dma("sp", hht[:], h_out[G(t0):G(t0) + 128, :], [h_out], [hht], hht)
                    sc.dma("sp", xt[:], xin[G(t0):G(t0) + 128, :], [xin], [xt], xt)
                    sc.dma("sp", mat[:], ma_sc[G(t0):G(t0) + 128, :], [ma_sc], [mat], mat)
                    for h in range(4):
                        sc.op("dve", lambda e, h=h: e.bn_stats(st[:, h, :], hht[:, h * 256:(h + 1) * 256]), reads=[hht], writes=[st])
                        sc.op("dve", lambda e, h=h: e.bn_aggr(mv[:, h, :], st[:, h, :]), reads=[st], writes=[mv])
                    sc.op("act", lambda e: e.activation(rs4[:], mv[:, :, 1], AF.Sqrt, bias=epsc[:, 0:1]), reads=[mv, epsb], writes=[rs4])
                    sc.op("dve", lambda e: e.reciprocal(rs4[:], rs4[:]), reads=[rs4], writes=[rs4])
                    for h in range(4):
                        sc.op("dve", lambda e, h=h: e.tensor_scalar(
                            hn[:, h * 256:(h + 1) * 256], hht[:, h * 256:(h + 1) * 256], mv[:, h, 0:1], rs4[:, h:h + 1],
                            ALU.subtract, ALU.mult), reads=[hht, mv, rs4], writes=[hn])
                    sc.op("dve", lambda e: e.tensor_tensor(hn[:], hn[:], mng[:], ALU.mult), reads=[hn, mng], writes=[hn])
                    for blk in range(2):
                        cs = slice(blk * 512, (blk + 1) * 512)
                        pom = nps()
                        mm_chain(pom, pom[:, :], [(xT[:, kc, t0:t0 + 128], w_om[:, kc, cs]) for kc in range(8)], [xT, w_om])
                        sc.op("dve", lambda e, pom=pom, cs=cs: e.tensor_tensor(so[:], pom[:, :], b_om[:, cs], ALU.add),
                              reads=[pom, b_om], writes=[so])
                        sc.op("act", lambda e: e.activation(so[:], so[:], AF.Sigmoid), reads=[so], writes=[so])
                        sc.op("dve", lambda e, cs=cs: e.tensor_tensor(hn[:, cs], hn[:, cs], so[:], ALU.mult), reads=[hn, so], writes=[hn])
                    transpose_tm(lambda kc: hn[:, kc * 128:(kc + 1) * 128], hbT, hn)
                    for blk in range(2):
                        cs = slice(blk * 512, (blk + 1) * 512)
                        pgb = nps()
                        mm_chain(pgb, pgb[:, :], [(xT[:, kc, t0:t0 + 128], w_gb[:, kc, cs]) for kc in range(8)], [xT, w_gb])
                        sc.op("dve", lambda e, pgb=pgb, cs=cs: e.tensor_tensor(so[:], pgb[:, :], b_gb[:, cs], ALU.add),
                              reads=[pgb, b_gb], writes=[so])
                        sc.op("act", lambda e: e.activation(so[:], so[:], AF.Sigmoid), reads=[so], writes=[so])
                        pyb = nps()
                        mm_chain(pyb, pyb[:, :], [(hbT[:, ec, :], w_b_sb[:, ec, cs]) for ec in range(8)], [hbT, w_b_sb])
                        sc.op("dve", lambda e, pyb=pyb, cs=cs: e.tensor_tensor(mg[:, cs], pyb[:, :], so[:], ALU.mult),
                              reads=[pyb, so], writes=[mg])
                        sc.op("dve", lambda e, cs=cs: e.tensor_tensor(mg[:, cs], mg[:, cs], mat[:, cs], ALU.add), reads=[mg, mat], writes=[mg])
                    transpose_tm(lambda kc: mg[:, kc * 128:(kc + 1) * 128], mT, mg)
                    for blk in range(2):
                        cs = slice(blk * 512, (blk + 1) * 512)
                        po = nps()
                        mm_chain(po, po[:, :], [(mT[:, ec, :], w_o_sb[:, ec, cs]) for ec in range(8)], [mT, w_o_sb])
                        sc.op("dve", lambda e, po=po, cs=cs: e.scalar_tensor_tensor(
                            hn[:, cs], xt[:, cs], ALPHA, po[:, :], ALU.mult, ALU.add), reads=[xt, po], writes=[hn])
                        sc.op("dve", lambda e, blk=blk, cs=cs: e.bn_stats(st2[:, blk, :], hn[:, cs]), reads=[hn], writes=[st2])
                    sc.op("dve", lambda e: e.bn_aggr(mv2[:], st2[:].rearrange("p a b -> p (a b)")), reads=[st2], writes=[mv2])
                    sc.op("act", lambda e: e.activation(rs1[:], mv2[:, 1:2], AF.Sqrt, bias=epsc[:, 0:1]), reads=[mv2, epsb], writes=[rs1])
                    sc.op("dve", lambda e: e.reciprocal(rs1[:], rs1[:]), reads=[rs1], writes=[rs1])
                    sc.op("dve", lambda e: e.tensor_scalar(hn[:], hn[:], mv2[:, 0:1], rs1[:, 0:1], ALU.subtract, ALU.mult),
                          reads=[hn, mv2, rs1], writes=[hn])
                    sc.op("dve", lambda e: e.tensor_tensor(hn[:], hn[:], l1g[:], ALU.mult), reads=[hn, l1g], writes=[hn])
                    sc.op("dve", lambda e: e.tensor_tensor(hn[:], hn[:], l1b[:], ALU.add), reads=[hn, l1b], writes=[hn])
                    sc.dma("sp", h1_sc[G(t0):G(t0) + 128, :], hn[:], [hn], [h1_sc], hn)
                sc.barrier()
                sc.end_scope(e4)
            if cfg.phaseD:
              with ExitStack() as e5:
                sb5 = lambda n, sh, dt=F32: sc.sbuf(n, sh, dt, e5)
                wpq = sb5("wpq", [128, 8, 2048])
                keysT = sb5("keysT", [128, 16, 128])
                kst = sb5("kst", [128, 128])
                l2g = sb5("l2g", [128, 1024]); l2b = sb5("l2b", [128, 1024])
                h1t = sb5("h1t", [128, 1024]); h1T = sb5("h1T", [128, 8, 128])
                qTj = [sb5(f"qTj{i}", [128, 128]) for i in range(2)]
                s_all = sb5("s_all", [128, 16, 128]); s2 = sb5("s2", [128, 128])
                sv = sb5("sv", [128, 16, 16]); si = sb5("si", [128, 16, 16], U32); sif = sb5("sif", [128, 16, 16])
                cand = sb5("cand", [128, 8, 256]); c2 = sb5("c2", [128, 256])
                top = sb5("top", [128, 8, 16]); pos = sb5("pos", [128, 8, 16], U32)
                pi_ = sb5("pi_", [128, 8, 16], U32); pj_ = sb5("pj_", [128, 8, 16], U32)
                pif = sb5("pif", [128, 8, 16]); pjf = sb5("pjf", [128, 8, 16])
                oh = sb5("oh", [128, 16, 16])
                e0 = sb5("e0", [128, 8, 16]); e1_ = sb5("e1_", [128, 8, 16]); ef = sb5("ef", [128, 8, 16])
                ei = sb5("ei", [128, 128], I32)
                nm = sb5("nm", [128, 8]); gexp = sb5("gexp", [128, 8, 16]); gsum = sb5("gsum", [128, 8]); rg = sb5("rg", [128, 8])
                gat = sb5("gat", [128, 8, 16])
                av = sb5("av", [128, 128]); wv_ = sb5("wv_", [128, 128])
                junk = sb5("junk", [128, 1024]); accv = sb5("accv", [128, 1024])
                NSL = 4
                Ug = [sb5(f"Ug{i}", [128, 1024]) for i in range(NSL)]
                Vg = [sb5(f"Vg{i}", [128, 1024]) for i in range(NSL)]
                st2 = sb5("st2d", [128, 2, 6]); mv2 = sb5("mv2d", [128, 2]); rs1 = sb5("rs1d", [128, 1])
                IO = cst[:, 24:26, :].rearrange("p a (k i) -> p (a k) i", i=16)
                sc.dma("sp", wpq[:], wpq_d[:, :].rearrange("(kc p) c -> p kc c", p=128), [wpq_d], [wpq], wpq)
                sc.dma("sp", l2g[:], vec_d[3:4, :].partition_broadcast(128), [vec_d], [l2g], l2g)
                sc.dma("sp", l2b[:], vec_d[4:5, :].partition_broadcast(128), [vec_d], [l2b], l2b)
                for j in range(16):
                    sc.dma("sp", kst[:], pk_d[j], [pk_d], [kst], kst)
                    pb = nps()
                    sc.op("pe", lambda e, pb=pb: e.transpose(pb[:, 0:128], kst[:], ident), reads=[kst, cst], writes=[pb])
                    sc.op("act", lambda e, pb=pb, j=j: e.copy(keysT[:, j, :], pb[:, 0:128]), reads=[pb], writes=[keysT])
                h1src = h1in_d if cfg.peer_only else h1_sc
                for ti in range(NTT):
                    t0 = tok0[ti]
                    sc.dma("sp", h1t[:], h1src[G(t0):G(t0) + 128, :], [h1src], [h1t], h1t)
                    for half in range(2):
                        pb = nps()
                        for j in range(4):
                            kc = half * 4 + j
                            sc.op("pe", lambda e, pb=pb, j=j, kc=kc: e.transpose(pb[:, j * 128:(j + 1) * 128], h1t[:, kc * 128:(kc + 1) * 128], ident),
                                  reads=[h1t, cst], writes=[pb] if j == 0 else [], chain=[] if j == 0 else [pb])
                        sc.op("act", lambda e, pb=pb, half=half: e.copy(h1T[:, half * 4:half * 4 + 4, :], pb[:, :].rearrange("p (j r) -> p j r", j=4)),
                              reads=[pb], writes=[h1T])
                    for j in range(16):
                        pq = nps()
                        mm_chain(pq, pq[:, 0:128], [(wpq[:, kc, j * 128:(j + 1) * 128], h1T[:, kc, :]) for kc in range(8)], [wpq, h1T])
                        qt = qTj[j % 2]
                        sc.op("act", lambda e, pq=pq, qt=qt: e.copy(qt[:], pq[:, 0:128]), reads=[pq], writes=[qt])
                        pss = nps()
                        mm_chain(pss, pss[:, 0:128], [(qt[:], keysT[:, j, :])], [qt, keysT])
                        sc.op("dve", lambda e, pss=pss, j=j: e.tensor_copy(s_all[:, j, :], pss[:, 0:128]), reads=[pss], writes=[s_all])
                    for j in range(16):
                        sc.op("dve", lambda e, j=j: e.max(out=sv[:, j, 0:8], in_=s_all[:, j, :]), reads=[s_all], writes=[sv])
                        sc.op("dve", lambda e, j=j: e.match_replace(out=s2[:], in_to_replace=sv[:, j, 0:8], in_values=s_all[:, j, :], imm_value=-1e30),
                              reads=[sv, s_all], writes=[s2])
                        sc.op("dve", lambda e, j=j: e.max(out=sv[:, j, 8:16], in_=s2[:]), reads=[s2], writes=[sv])
                        sc.op("dve", lambda e, j=j: e.max_index(out=si[:, j, 0:8], in_max=sv[:, j, 0:8], in_values=s_all[:, j, :]),
                              reads=[sv, s_all], writes=[si])
                        sc.op("dve", lambda e, j=j: e.max_index(out=si[:, j, 8:16], in_max=sv[:, j, 8:16], in_values=s_all[:, j, :]),
                              reads=[sv, s_all], writes=[si])
                    sc.op("dve", lambda e: e.tensor_copy(sif[:], si[:]), reads=[si], writes=[sif])
                    for h in range(8):
                        sc.op("dve", lambda e, h=h: e.tensor_tensor(
                            cand[:, h, :].rearrange("p (i j) -> p i j", j=16),
                            sv[:, 2 * h, :].unsqueeze(2).to_broadcast([128, 16, 16]),
                            sv[:, 2 * h + 1, :].unsqueeze(1).to_broadcast([128, 16, 16]), ALU.add), reads=[sv], writes=[cand])
                    for h in range(8):
                        sc.op("dve", lambda e, h=h: e.max(out=top[:, h, 0:8], in_=cand[:, h, :]), reads=[cand], writes=[top])
                        sc.op("dve", lambda e, h=h: e.match_replace(out=c2[:], in_to_replace=top[:, h, 0:8], in_values=cand[:, h, :], imm_value=-1e30),
                              reads=[top, cand], writes=[c2])
                        sc.op("dve", lambda e, h=h: e.max(out=top[:, h, 8:16], in_=c2[:]), reads=[c2], writes=[top])
                        sc.op("dve", lambda e, h=h: e.max_index(out=pos[:, h, 0:8], in_max=top[:, h, 0:8], in_values=cand[:, h, :]),
                              reads=[top, cand], writes=[pos])
                        sc.op("dve", lambda e, h=h: e.max_index(out=pos[:, h, 8:16], in_max=top[:, h, 8:16], in_values=cand[:, h, :]),
                              reads=[top, cand], writes=[pos])
                    sc.op("dve", lambda e: e.tensor_scalar(nm[:], top[:, :, 0], -1.0, None, ALU.mult), reads=[top], writes=[nm])
                    for h in range(8):
                        sc.op("act", lambda e, h=h: e.activation(gexp[:, h, :], top[:, h, :], AF.Exp, bias=nm[:, h:h + 1]),
                              reads=[top, nm], writes=[gexp])
                    sc.op("dve", lambda e: e.tensor_reduce(gsum[:], gexp[:], AX.X, ALU.add), reads=[gexp], writes=[gsum])
                    sc.op("dve", lambda e: e.reciprocal(rg[:], gsum[:]), reads=[gsum], writes=[rg])
                    sc.op("dve", lambda e: e.tensor_tensor(gat[:], gexp[:], rg[:].unsqueeze(2).to_broadcast([128, 8, 16]), ALU.mult),
                          reads=[gexp, rg], writes=[gat])
                    sc.op("dve", lambda e: e.tensor_scalar(pi_[:], pos[:], 4, None, ALU.logical_shift_right), reads=[pos], writes=[pi_])
                    sc.op("dve", lambda e: e.tensor_scalar(pj_[:], pos[:], 15, None, ALU.bitwise_and), reads=[pos], writes=[pj_])
                    sc.op("dve", lambda e: e.tensor_copy(pif[:], pi_[:]), reads=[pi_], writes=[pif])
                    sc.op("dve", lambda e: e.tensor_copy(pjf[:], pj_[:]), reads=[pj_], writes=[pjf])
                    for h in range(8):
                        for (pf, jj, eo) in ((pif, 2 * h, e0), (pjf, 2 * h + 1, e1_)):
                            sc.op("dve", lambda e, pf=pf, h=h: e.tensor_tensor(
                                oh[:], pf[:, h, :].unsqueeze(2).to_broadcast([128, 16, 16]), IO, ALU.is_equal),
                                reads=[pf, cst], writes=[oh])
                            sc.op("dve", lambda e, jj=jj: e.tensor_tensor(
                                oh[:], oh[:], sif[:, jj, :].unsqueeze(1).to_broadcast([128, 16, 16]), ALU.mult),
                                reads=[oh, sif], writes=[oh])
                            sc.op("dve", lambda e, eo=eo, h=h: e.tensor_reduce(eo[:, h, :], oh[:], AX.X, ALU.add), reads=[oh], writes=[eo])
                    sc.op("dve", lambda e: e.scalar_tensor_tensor(ef[:], e0[:], 128.0, e1_[:], ALU.mult, ALU.add), reads=[e0, e1_], writes=[ef])
                    sc.op("dve", lambda e: e.tensor_copy(ei[:], ef[:].rearrange("p h k -> p (h k)")), reads=[ef], writes=[ei])
                    if cfg.debug:
                        sc.dma("sp", dbg_ei[G(t0):G(t0) + 128, :], ei[:], [ei], [dbg_ei], ei)
                        sc.dma("sp", dbg_g[G(t0):G(t0) + 128, :], gat[:].rearrange("p h k -> p (h k)"), [gat], [dbg_g], gat)
                    for hk in range(128):
                        ug = Ug[hk % NSL]
                        sc.dma("pool", ug[:], pu_d[:, :], [pu_d, ei], [ug], ug,
                               indirect=dict(out_offset=None, in_offset=bass.IndirectOffsetOnAxis(ap=ei[:, hk:hk + 1], axis=0)))
                        sc.op("dve", lambda e, ug=ug, hk=hk: e.scalar_tensor_tensor(
                            junk[:], ug[:], 1.0, h1t[:], ALU.mult, ALU.mult, accum_out=av[:, hk:hk + 1]),
                            reads=[ug, h1t], writes=[junk, av])
                    sc.op("act", lambda e: e.activation(wv_[:], av[:], AF.Gelu), reads=[av], writes=[wv_])
                    sc.op("dve", lambda e: e.tensor_tensor(wv_[:], wv_[:], gat[:].rearrange("p h k -> p (h k)"), ALU.mult),
                          reads=[wv_, gat], writes=[wv_])
                    for hk in range(128):
                        vg = Vg[hk % NSL]
                        sc.dma("pool", vg[:], pv_d[:, :], [pv_d, ei], [vg], vg,
                               indirect=dict(out_offset=None, in_offset=bass.IndirectOffsetOnAxis(ap=ei[:, hk:hk + 1], axis=0)))
                        if hk == 0:
                            sc.op("dve", lambda e, vg=vg: e.tensor_scalar(accv[:], vg[:], wv_[:, 0:1], None, ALU.mult),
                                  reads=[vg, wv_], writes=[accv])
                        else:
                            sc.op("dve", lambda e, vg=vg, hk=hk: e.scalar_tensor_tensor(
                                accv[:], vg[:], wv_[:, hk:hk + 1], accv[:], ALU.mult, ALU.add), reads=[vg, wv_, accv], writes=[accv])
                    sc.op("dve", lambda e: e.scalar_tensor_tensor(accv[:], h1t[:], ALPHA, accv[:], ALU.mult, ALU.add),
                          reads=[h1t, accv], writes=[accv])
                    for blk in range(2):
                        sc.op("dve", lambda e, blk=blk: e.bn_stats(st2[:, blk, :], accv[:, blk * 512:(blk + 1) * 512]), reads=[accv], writes=[st2])
                    sc.op("dve", lambda e: e.bn_aggr(mv2[:], st2[:].rearrange("p a b -> p (a b)")), reads=[st2], writes=[mv2])
                    sc.op("act", lambda e: e.activation(rs1[:], mv2[:, 1:2], AF.Sqrt, bias=epsc[:, 0:1]), reads=[mv2, epsb], writes=[rs1])
                    sc.op("dve", lambda e: e.reciprocal(rs1[:], rs1[:]), reads=[rs1], writes=[rs1])
                    sc.op("dve", lambda e: e.tensor_scalar(accv[:], accv[:], mv2[:, 0:1], rs1[:, 0:1], ALU.subtract, ALU.mult),
                          reads=[accv, mv2, rs1], writes=[accv])
                    sc.op("dve", lambda e: e.tensor_tensor(accv[:], accv[:], l2g[:], ALU.mult), reads=[accv, l2g], writes=[accv])
                    sc.op("dve", lambda e: e.tensor_tensor(accv[:], accv[:], l2b[:], ALU.add), reads=[accv, l2b], writes=[accv])
                    sc.dma("sp", y_out[G(t0):G(t0) + 128, :], accv[:], [accv], [y_out], accv, is_output=True)
                sc.barrier()
                sc.end_scope(e5)
        for jb in range(NPS):
            run_job(jb, jb == 0)
        sc.finish()
    return nc


def host_inputs(inp, c, cfg):
    S, NPS, NSQ = cfg.S, cfg.NPS, cfg.NSQ
    f = lambda k: np.asarray(inp[k], np.float32)
    xp, xs = f("x_prompt"), f("x_sample")
    DS = xs.shape[1]
    xin = np.zeros((NPS * S + 128, D), np.float32)
    xin[:NPS * S] = xp[c * NPS:(c + 1) * NPS].reshape(NPS * S, D)
    xin[NPS * S:NPS * S + NSQ * DS] = xs[c * NSQ:(c + 1) * NSQ].reshape(NSQ * DS, D)
    b_in = f("b_in")[0]
    smallc = np.zeros((128, 128), np.float32)
    smallc[:, 0:8] = b_in[3072:4096].reshape(8, 128).T
    wc = f("w_conv")[0]
    smallc[:, 8:40] = wc.reshape(4, 8, 128).transpose(2, 1, 0).reshape(128, 32)
    smallc[:, 40:48] = f("b_conv")[0].reshape(8, 128).T
    for j in range(NSQ):
        smallc[4 * j:4 * j + 4, 48 + j] = 1.0
    smallc[:, 56:72] = b_in[0:2048].reshape(16, 128).T
    smallc[0:64, 72:88] = b_in[0:1024].reshape(16, 64).T
    smallc[:, 88] = np.arange(128)
    sm = f("state_m")[0][c * NSQ:(c + 1) * NSQ]
    mst = np.zeros((128, 4 + 4 * NSQ), np.float32)
    for j in range(NSQ):
        mst[4 * j:4 * j + 4, 0:4] = sm[j]
        mst[:, 4 + 4 * j:8 + 4 * j] = sm[j]
    cv = f("state_conv")[0][c * NSQ:(c + 1) * NSQ]
    convT = np.ascontiguousarray(cv.reshape(NSQ, 3, 8, 128).transpose(3, 2, 0, 1))
    return {
        "xin": xin, "w_in": np.ascontiguousarray(f("w_in")[0]), "b_in": np.ascontiguousarray(b_in[None, :]),
        "consts": make_consts(NSQ), "smallc": smallc, "mst": mst,
        "w_a": np.ascontiguousarray(f("w_a")[0]), "w_b": np.ascontiguousarray(f("w_b")[0]), "w_o": np.ascontiguousarray(f("w_o")[0]),
        "subln_g": np.ascontiguousarray(f("subln_g")[0][None, :]),
        "vecs": np.stack([f("mnorm_g")[0], f("ln1_g")[0], f("ln1_b")[0], f("ln2_g")[0], f("ln2_b")[0],
                          np.zeros(D, np.float32), np.zeros(D, np.float32), np.zeros(D, np.float32)]),
        "w_pq": np.ascontiguousarray(f("w_pq")[0]), "p_keys": np.ascontiguousarray(f("p_keys")[0].reshape(16, 128, 128)),
        "p_u": np.ascontiguousarray(f("p_u")[0]), "p_v": np.ascontiguousarray(f("p_v")[0]),
        "lam": np.stack([f("lam_q1")[0], f("lam_k1")[0], f("lam_q2")[0], f("lam_k2")[0]]),
        "w_qm": np.ascontiguousarray(f("w_qm")[0]), "w_km": np.ascontiguousarray(f("w_km")[0]),
        "st_C": np.ascontiguousarray(f("state_C")[0][c * NSQ:(c + 1) * NSQ]),
        "st_n": np.ascontiguousarray(f("state_n")[0][c * NSQ:(c + 1) * NSQ]),
        "st_convT": convT,
        "cache_k": np.asarray(inp["cache_k"], np.float32)[0].reshape(-1, D),
        "cache_v": np.asarray(inp["cache_v"], np.float32)[0].reshape(-1, D),
        "ptab": np.ascontiguousarray(np.asarray(inp["page_table"], np.int32)[c * NSQ:(c + 1) * NSQ].reshape(1, -1)),
    }


def assemble(R, inp, cfg):
    S, NPS, NSQ = cfg.S, cfg.NPS, cfg.NSQ
    xs = np.asarray(inp["x_sample"])
    nco = len(R)
    DS = xs.shape[1]
    B, DB = nco * NPS, nco * NSQ

    def gather(name):
        p = np.concatenate([R[c][name][:NPS * S].reshape(NPS, S, D) for c in range(nco)])
        s = np.concatenate([R[c][name][NPS * S:NPS * S + NSQ * DS].reshape(NSQ, DS, D) for c in range(nco)])
        return p, s
    kp, ks = gather("k_out")
    vp, vs = gather("v_out")
    up, us = gather("conv_out")
    y_p, y_s = gather("y_out")
    k_p = kp.reshape(1, B, S, 8, 128); v_p = vp.reshape(1, B, S, 8, 128)
    k_s = ks.reshape(1, DB, DS, 8, 128); v_s = vs.reshape(1, DB, DS, 8, 128)
    conv_p = np.ascontiguousarray(up[:, S - 3:, :]).reshape(1, B, 3, D)
    conv_s = np.ascontiguousarray(us[:, DS - 3:, :]).reshape(1, DB, 3, D)
    cat = lambda name, sl: np.concatenate([R[c][name][sl] for c in range(nco)])[None]
    C_p, n_p, m_p = cat("C_out", slice(0, NPS)), cat("n_out", slice(0, NPS)), cat("m_out", slice(0, NPS))
    C_s, n_s, m_s = cat("C_out", slice(NPS, None)), cat("n_out", slice(NPS, None)), cat("m_out", slice(NPS, None))
    return (y_p, y_s, k_p, v_p, C_p, n_p, m_p, conv_p, k_s, v_s, C_s, n_s, m_s, conv_s)


def kernel(**inp):
    cfg = Cfg()
    cfg.NPOOL = int(np.asarray(inp["cache_k"]).shape[1])
    nc = build(cfg)
    in_maps = [host_inputs(inp, c, cfg) for c in range(cfg.NCORES)]
    res = run_bass_kernel_spmd(nc, in_maps, core_ids=list(range(cfg.NCORES)))
    return assemble(res.results, inp, cfg)
```

```python
import numpy as np
from contextlib import ExitStack
import concourse.bass as bass
import concourse.mybir as mybir
from concourse.bass_utils import run_bass_kernel_spmd

F32 = mybir.dt.float32
BF16 = mybir.dt.bfloat16
I32 = mybir.dt.int32
U32 = mybir.dt.uint32
AF = mybir.ActivationFunctionType
ALU = mybir.AluOpType
AX = mybir.AxisListType

D = 1024
D_IN = 8200
NCORES = 8
EPOCH = 8000
DMA_RENEW = 500


class Cfg:
    S = 2048
    NCORES = 4
    NPS = 2
    NSQ = 8
    debug = False
    attn = True
    phaseS = True
    PAST = 8192
    NPOOL = 2560
    phaseD = True
    peer_only = False
    phaseC = True
    attn_c1 = True
    a_lam = True
    a_qk = True
    a_vx = True


class Buf:
    __slots__ = ("t", "name", "w", "r", "dsem", "dcnt", "dsems")

    def __init__(self, t, name):
        self.t = t
        self.name = name
        self.w = None
        self.r = []
        self.dsem = None
        self.dcnt = 0
        self.dsems = []

    def __getitem__(self, k):
        return self.t[k]


class Sched:
    def __init__(self, nc, es):
        self.nc = nc
        self.es = es
        self.eng = {"pe": nc.tensor, "dve": nc.vector, "act": nc.scalar, "pool": nc.gpsimd, "sp": nc.sync}
        self.cnt = {e: 0 for e in self.eng}
        self.sems = {e: [] for e in self.eng}
        self.seen = {e: {} for e in self.eng}
        self.nsem = 0
        self.out_events = []
        self.dma_events = {}
        self.scoped = {}
        self.free_dsems = []

    def new_sem(self, name):
        self.nsem += 1
        return self.es.enter_context(self.nc.semaphore(f"{name}_{self.nsem}"))

    def sbuf(self, name, shape, dt, es=None):
        self.nsem += 1
        name = f"{name}_u{self.nsem}"
        t = (es or self.es).enter_context(self.nc.sbuf_tensor(name, list(shape), dt))
        b = Buf(t, name)
        if es is not None:
            self.scoped.setdefault(id(es), []).append(b)
        return b

    def end_scope(self, es):
        for b in self.scoped.pop(id(es), []):
            if b.dsem is not None and b.dcnt < 16 * DMA_RENEW:
                self.free_dsems.append((b.dsem, b.dcnt))
            b.dsem = None

    def barrier(self):
        evs = list(self.dma_events.values())
        for e2 in self.eng:
            c = self.cnt[e2]
            if c:
                ep = (c - 1) // EPOCH
                evs.append((self.sems[e2][ep], c - ep * EPOCH))
        for e in self.eng:
            for ev in evs:
                self._wait(e, ev)
        self.dma_events = {}

    def psum(self, name, shape, dt):
        t = self.es.enter_context(self.nc.psum_tensor(name, list(shape), dt))
        return Buf(t, name)

    def dram(self, name, shape, dt, kind):
        t = self.nc.dram_tensor(name, list(shape), dt, kind=kind)
        return Buf(t.ap(), name)

    def _wait(self, e, ev):
        if ev is None:
            return
        sem, val = ev
        k = id(sem)
        if self.seen[e].get(k, 0) >= val:
            return
        self.eng[e].wait_ge(sem, val)
        self.seen[e][k] = val

    def _deps(self, e, reads, writes):
        for b in reads:
            self._wait(e, b.w)
        for b in writes:
            self._wait(e, b.w)
            for ev in b.r:
                self._wait(e, ev)

    def op(self, e, fn, reads=(), writes=(), chain=()):
        self._deps(e, reads, writes)
        writes = list(writes) + list(chain)
        ins = fn(self.eng[e])
        self.cnt[e] += 1
        c = self.cnt[e]
        ep = (c - 1) // EPOCH
        while len(self.sems[e]) <= ep:
            self.sems[e].append(self.new_sem(f"s{e}"))
        sem = self.sems[e][ep]
        val = c - ep * EPOCH
        ins.then_inc(sem, 1)
        ev = (sem, val)
        self.seen[e][id(sem)] = max(self.seen[e].get(id(sem), 0), 0)
        for b in reads:
            b.r.append(ev)
        for b in writes:
            b.w = ev
            b.r = []
        return ev

    def dma(self, q, out, in_, reads, writes, key, indirect=None, is_output=False, slow=False):
        self._deps(q, reads, writes)
        if key.dsem is None or key.dcnt >= 16 * DMA_RENEW:
            if self.free_dsems:
                key.dsem, key.dcnt = self.free_dsems.pop()
            else:
                key.dsem = self.new_sem(f"d{key.name}")
                key.dcnt = 0
        if indirect is None:
            ins = self.eng[q].dma_start(out=out, in_=in_, allow_slow_non_contiguous=True) if slow else self.eng[q].dma_start(out=out, in_=in_)
        else:
            ins = self.eng[q].indirect_dma_start(out=out, in_=in_, **indirect)
        key.dcnt += 16
        ins.then_inc(key.dsem, 16)
        ev = (key.dsem, key.dcnt)
        self.dma_events[id(key.dsem)] = ev
        for b in reads:
            b.r.append(ev)
        for b in writes:
            b.w = ev
            b.r = []
        if is_output:
            self.out_events.append(ev)
        return ev

    def collective(self, kind, op, groups, src, dst):
        self._deps("pool", [src], [dst])
        if dst.dsem is None:
            dst.dsem = self.new_sem(f"cc{dst.name}")
            dst.dcnt = 0
        ins = self.eng["pool"].collective_compute(kind, op, replica_groups=groups, ins=[src[:]], outs=[dst[:]])
        dst.dcnt += 16
        ins.then_inc(dst.dsem, 16)
        ev = (dst.dsem, dst.dcnt)
        self.dma_events[id(dst.dsem)] = ev
        src.r.append(ev)
        dst.w = ev
        dst.r = []
        return ev

    def finish(self):
        last = {}
        for sem, val in self.out_events:
            k = id(sem)
            if k not in last or last[k][1] < val:
                last[k] = (sem, val)
        for ev in last.values():
            self._wait("sp", ev)


LAM_INIT = 0.2
ALPHA = 2.0 ** 0.25
NCONST = 27


def make_consts(NSQ):
    c = np.zeros((NCONST, 128, 128), np.float32)
    k = np.arange(128)[:, None]
    t = np.arange(128)[None, :]
    nv = 4 * NSQ
    c[0] = np.eye(128)
    c[1] = 1.0
    c[2] = (k <= t)
    same = (k // 4 == t // 4) & (k < nv) & (t < nv)
    c[3] = (same & (k <= t)) | (k == t)
    c[4] = np.where(t <= k, 0.0, -1e30)
    c[5] = np.where((same & (t <= k)) | (k == t), 0.0, -1e30)
    c[6] = (k == 127) * np.ones((1, 128))
    for j in range(NSQ):
        c[7 + j] = (k == 4 * j + 3) * np.ones((1, 128))
    p = t
    c[15] = np.where(p < nv, k == 4 * (p // 4) + 3, k == p)
    for j in range(NSQ):
        c[16 + j] = ((t >= 4 * j) & (t < 4 * j + 4)) * np.ones((128, 1))
    c[24] = (np.arange(128) % 16)[None, :]
    c[25] = (np.arange(128) % 16)[None, :]
    c[26] = np.where(k <= (t % 4), 0.0, -30000.0)
    return np.ascontiguousarray(c.transpose(1, 0, 2))


def build(cfg):
    nc = bass.Bass("TRN2", target_bir_lowering=False)
    S = cfg.S
    NT = S // 128
    NPS, NSQ = cfg.NPS, cfg.NSQ
    T = S + 128
    TD = NPS * S + 128

    with ExitStack() as es:
        sc = Sched(nc, es)
        xin = sc.dram("xin", [TD, D], F32, "ExternalInput")
        w_in = sc.dram("w_in", [D, D_IN], F32, "ExternalInput")
        b_in = sc.dram("b_in", [1, D_IN], F32, "ExternalInput")
        consts_d = sc.dram("consts", [128, NCONST, 128], F32, "ExternalInput")
        smallc_d = sc.dram("smallc", [128, 128], F32, "ExternalInput")
        mst_d = sc.dram("mst", [128, 4 + 4 * NSQ], F32, "ExternalInput")
        wqm_d = sc.dram("w_qm", [4, 256, 256], F32, "ExternalInput")
        wkm_d = sc.dram("w_km", [4, 256, 256], F32, "ExternalInput")
        stC_d = sc.dram("st_C", [NSQ, 4, 256, 256], F32, "ExternalInput")
        stn_d = sc.dram("st_n", [NSQ, 4, 256], F32, "ExternalInput")
        convT_d = sc.dram("st_convT", [128, 8, NSQ, 3], F32, "ExternalInput")
        lam_d = sc.dram("lam", [4, 64], F32, "ExternalInput")
        SCR = "ExternalOutput" if cfg.debug else "Internal"
        oa_sc = sc.dram("oa_sc", [TD, D], F32, SCR)
        ma_sc = sc.dram("ma_sc", [TD, D], F32, SCR)
        h1_sc = sc.dram("h1_sc", [TD, D], F32, SCR)
        wpq_d = sc.dram("w_pq", [D, 2048], F32, "ExternalInput")
        pk_d = sc.dram("p_keys", [16, 128, 128], F32, "ExternalInput")
        pu_d = sc.dram("p_u", [16384, D], F32, "ExternalInput")
        pv_d = sc.dram("p_v", [16384, D], F32, "ExternalInput")
        y_out = sc.dram("y_out", [TD, D], F32, "ExternalOutput")
        pub_d = sc.dram("pu_bf16", [16384, D], BF16, "Internal")
        pvb_d = sc.dram("pv_bf16", [16384, D], BF16, "Internal")
        if cfg.peer_only:
            h1in_d = sc.dram("h1_in", [TD, D], F32, "ExternalInput")
        if cfg.debug:
            dbg_ei = sc.dram("dbg_ei", [TD, 128], I32, "ExternalOutput")
            dbg_g = sc.dram("dbg_g", [TD, 128], F32, "ExternalOutput")
        NPP = cfg.PAST // 128
        ck_d = sc.dram("cache_k", [cfg.NPOOL * 128, D], F32, "ExternalInput")
        cv_d = sc.dram("cache_v", [cfg.NPOOL * 128, D], F32, "ExternalInput")
        ptab_d = sc.dram("ptab", [1, NSQ * NPP], I32, "ExternalInput")
        wa_d = sc.dram("w_a", [D, D], F32, "ExternalInput")
        wb_d = sc.dram("w_b", [D, D], F32, "ExternalInput")
        wo_d = sc.dram("w_o", [D, D], F32, "ExternalInput")
        subg_d = sc.dram("subln_g", [1, 128], F32, "ExternalInput")
        vec_d = sc.dram("vecs", [8, D], F32, "ExternalInput")
        k_out = sc.dram("k_out", [TD, D], F32, "ExternalOutput")
        v_out = sc.dram("v_out", [TD, D], F32, "ExternalOutput")
        conv_out = sc.dram("conv_out", [TD, D], F32, "ExternalOutput")
        h_out = sc.dram("h_out", [TD, D], F32, "ExternalOutput")
        C_out = sc.dram("C_out", [NPS + NSQ, 4, 256, 256], F32, "ExternalOutput")
        n_out = sc.dram("n_out", [NPS + NSQ, 4, 256], F32, "ExternalOutput")
        m_out = sc.dram("m_out", [NPS + NSQ, 4], F32, "ExternalOutput")

        cst = sc.sbuf("cst", [128, NCONST, 128], F32)
        smallc = sc.sbuf("smallc_sb", [128, 128], F32)
        xT = sc.sbuf("xT", [128, 8, T], BF16)
        ps = [sc.psum(f"ps{i}", [128, 512], F32) for i in range(8)]
        psi = [0]

        def nps():
            psi[0] += 1
            return ps[psi[0] % 8]

        def nps2():
            psi[0] += 1
            return ps[psi[0] % 2]

        sc.dma("sp", cst[:], consts_d[:], [consts_d], [cst], cst)
        sc.dma("sp", smallc[:], smallc_d[:], [smallc_d], [smallc], smallc)
        epsb = sc.sbuf("epsb", [128, 1], F32)
        sc.op("pool", lambda e: e.memset(epsb[:], 1e-5), writes=[epsb])
        epsc = epsb
        lams = sc.sbuf("lams", [128, 4], F32)
        with ExitStack() as e0:
            lamv = sc.sbuf("lamv", [128, 4, 64], F32, e0)
            lamw = sc.sbuf("lamw", [128, 2, 64], F32, e0)
            for i in range(4):
                sc.dma("sp", lamv[:, i, :], lam_d[i:i + 1, :].partition_broadcast(128), [lam_d], [lamv], lamv)
            sc.op("dve", lambda e: e.tensor_tensor(lamw[:, 0, :], lamv[:, 0, :], lamv[:, 1, :], ALU.mult), reads=[lamv], writes=[lamw])
            sc.op("dve", lambda e: e.tensor_tensor(lamw[:, 1, :], lamv[:, 2, :], lamv[:, 3, :], ALU.mult), reads=[lamv, lamw], writes=[lamw])
            sc.op("dve", lambda e: e.tensor_reduce(lams[:, 0:2], lamw[:], AX.X, ALU.add), reads=[lamw], writes=[lams])
            sc.op("act", lambda e: e.activation(lams[:, 0:2], lams[:, 0:2], AF.Exp), reads=[lams], writes=[lams])
            sc.op("dve", lambda e: e.tensor_tensor(lams[:, 2:3], lams[:, 0:1], lams[:, 1:2], ALU.subtract), reads=[lams], writes=[lams])
            sc.op("dve", lambda e: e.tensor_scalar(lams[:, 3:4], lams[:, 2:3], LAM_INIT, -1.0, ALU.add, ALU.mult), reads=[lams], writes=[lams])
            sc.barrier()
            sc.end_scope(e0)
        ident = cst[:, 0, :]
        ones = cst[:, 1, :]

        def mm_chain(pb, out_ap, pairs, reads, first_group=True):
            n = len(pairs)
            for i, (l, r) in enumerate(pairs):
                first = (i == 0)
                sc.op("pe", lambda e, l=l, r=r, i=i: e.matmul(out_ap, lhsT=l, rhs=r, start=(i == 0), stop=(i == n - 1)),
                      reads=reads, writes=[pb] if (first and first_group) else [],
                      chain=[] if (first and first_group) else [pb])

        def run_job(jb, has_sample):
            NTT = NT + (1 if has_sample else 0)
            T = NTT * 128
            tok0 = [t * 128 for t in range(NTT)]
            rb = jb * S
            G = lambda t0: (rb + t0) if t0 < S else (NPS * S + (t0 - S))
            with ExitStack() as e1:
              if not cfg.peer_only:
                  sb1 = lambda n, sh, dt=F32: sc.sbuf(n, sh, dt, e1)
                  xst = [sb1("xst0", [128, D])] * 2
                  wst = [sb1("wst0", [128, 8, 512])] * 2
                  wbf = [sb1("wbf0", [128, 8, 512], BF16)] * 2
                  bbc = [sb1(f"bbc{i}", [128, 512]) for i in range(2)]
                  ost = [sb1(f"ost{i}", [128, 512]) for i in range(3)]
                  QT = sb1("QT", [128, 8, T], BF16)
                  KT = sb1("KT", [128, 8, T], BF16)
                  Vx = sb1("Vx", [128, NT, 8, 129], BF16)
                  bq8 = sb1("bq8", [128, 8])
                  PT = [sb1(f"PT{i}", [128, 512], BF16) for i in range(4)]
                  oat = [sb1("oat0", [128, 8, 128])] * 2
                  rz = sb1("rz", [128, 2])
                  bqk = smallc[:, 56:72]
                  cmask = cst[:, 2, :]
                  sc.op("dve", lambda e: e.tensor_scalar(bq8[:], bqk[:, 0:8], 0.125, None, ALU.mult), reads=[smallc], writes=[bq8])
                  sc.op("pool", lambda e: e.memset(Vx[:], 1.0), writes=[Vx])
                  for ti in range(NTT):
                      t0 = tok0[ti]
                      xs = xst[ti % 2]
                      sc.dma("sp", xs[:, :], xin[G(t0):G(t0) + 128, :], [xin], [xs], xs)
                      for half in range(2):
                          pb = nps()
                          for j in range(4):
                              kc = half * 4 + j
                              sc.op("pe", lambda e, pb=pb, j=j, kc=kc, xs=xs: e.transpose(
                                  pb[:, j * 128:(j + 1) * 128], xs[:, kc * 128:(kc + 1) * 128], ident),
                                  reads=[xs, cst], writes=[pb] if j == 0 else [], chain=[] if j == 0 else [pb])
                          src = pb[:, :].rearrange("p (j r) -> p j r", j=4)
                          dst = xT[:, half * 4:half * 4 + 4, t0:t0 + 128]
                          if half == 0:
                              sc.op("act", lambda e, dst=dst, src=src: e.copy(dst, src), reads=[pb], writes=[xT])
                          else:
                              sc.op("dve", lambda e, dst=dst, src=src: e.tensor_copy(dst, src), reads=[pb], writes=[xT])
                  bi = 0

                  def load_blk(cc):
                      w_s, w_b = wst[bi % 2], wbf[bi % 2]
                      sc.dma("sp", w_s[:], w_in[:, cc:cc + 512].rearrange("(kc p) c -> p kc c", p=128), [w_in], [w_s], w_s)
                      sc.op("pool", lambda e, w_b=w_b, w_s=w_s: e.tensor_copy(w_b[:], w_s[:]), reads=[w_s], writes=[w_b])
                      return w_b
                  groups = [(g0, min(512, T - g0)) for g0 in range(0, T, 512)]
                  for blk in range(4 if cfg.a_qk else 0):
                      isq = blk < 2
                      w_b = load_blk(blk * 512)
                      bi += 1
                      for hh_ in range(4):
                          h = (blk % 2) * 4 + hh_
                          dstT = QT if isq else KT
                          for (g0, gn) in groups:
                              pb = nps()
                              mm_chain(pb, pb[:, 0:gn], [(w_b[:, kc, hh_ * 128:(hh_ + 1) * 128], xT[:, kc, g0:g0 + gn]) for kc in range(8)],
                                       [xT, w_b])
                              if isq:
                                  sc.op("act", lambda e, pb=pb, h=h, g0=g0, gn=gn: e.activation(
                                      QT[:, h, g0:g0 + gn], pb[:, 0:gn], AF.Identity, bias=bq8[:, h:h + 1], scale=0.125),
                                      reads=[pb, bq8], writes=[QT])
                              else:
                                  sc.op("act", lambda e, pb=pb, h=h, g0=g0, gn=gn: e.activation(
                                      KT[:, h, g0:g0 + gn], pb[:, 0:gn], AF.Identity, bias=bqk[:, 8 + h:9 + h]),
                                      reads=[pb, smallc], writes=[KT])
                  tm_blocks = [("k", 1024, 1024, k_out), ("v", 2048, 1024, v_out), ("u", 3072, 1024, conv_out)]
                  for name, c0, ncols, odram in tm_blocks:
                      for cb in range(0, ncols, 512):
                          b_b = bbc[bi % 2]
                          cc = c0 + cb
                          w_b = load_blk(cc)
                          sc.dma("sp", b_b[:], b_in[0:1, cc:cc + 512].partition_broadcast(128), [b_in], [b_b], b_b)
                          for ti in range(NTT):
                              t0 = tok0[ti]
                              pb = nps()
                              mm_chain(pb, pb[:, :], [(xT[:, kc, t0:t0 + 128], w_b[:, kc, :]) for kc in range(8)], [xT, w_b])
                              o = ost[(bi * NTT + ti) % 3]
                              sc.op("dve", lambda e, o=o, pb=pb, b_b=b_b: e.tensor_tensor(
                                  o[:, :], pb[:, :], b_b[:, :], ALU.add), reads=[pb, b_b], writes=[o])
                              if name == "v" and ti < NT and cfg.a_vx:
                                  hb0 = cb // 128
                                  sc.op("act", lambda e, o=o, ti=ti, hb0=hb0: e.copy(
                                      Vx[:, ti, hb0:hb0 + 4, 0:128], o[:, :].rearrange("p (h e) -> p h e", h=4)),
                                      reads=[o], writes=[Vx])
                              sc.dma("sp", odram[G(t0):G(t0) + 128, cb:cb + 512], o[:, :], [o], [odram], o, is_output=True)
                          bi += 1
                  psS = ps[0:4]
                  psO = ps[4:8]
                  si = 0
                  pti = 0
                  for qi in range(NT if cfg.attn else 0):
                      q0 = tok0[qi]
                      ot = oat[qi % 2]
                      for h in range(8):
                          acc = [psO[(h % 2) * 2 + c] for c in range(2)]
                          for kg in range(0, qi + 1, 4):
                              kis = list(range(kg, min(kg + 4, qi + 1)))
                              n = len(kis)
                              bankc = [psS[si % 4], psS[(si + 1) % 4]]; si += 2
                              ptc = [PT[pti % 4], PT[(pti + 1) % 4]]; pti += 2
                              for c in range(2):
                                  for idx, ki in enumerate(kis):
                                      k0 = tok0[ki]
                                      mm_chain(bankc[c], bankc[c][:, idx * 128:(idx + 1) * 128],
                                               [(KT[c * 64:(c + 1) * 64, h, k0:k0 + 128], QT[c * 64:(c + 1) * 64, h, q0:q0 + 128])],
                                               [KT, QT], first_group=(idx == 0))
                              for c in range(2):
                                  sc.op("act", lambda e, c=c, ptc=ptc, bankc=bankc, n=n: e.activation(
                                      ptc[c][:, 0:n * 128], bankc[c][:, 0:n * 128], AF.Exp), reads=[bankc[c]], writes=[ptc[c]])
                              if qi in kis:
                                  idx = qi - kg
                                  for c in range(2):
                                      sc.op("dve", lambda e, ptc=ptc, c=c, idx=idx: e.tensor_tensor(
                                          ptc[c][:, idx * 128:(idx + 1) * 128], ptc[c][:, idx * 128:(idx + 1) * 128], cmask, ALU.mult),
                                          reads=[ptc[c], cst], writes=[ptc[c]])
                              for idx, ki in enumerate(kis):
                                  for c in range(2):
                                      first = (ki == 0)
                                      sc.op("pe", lambda e, c=c, ptc=ptc, idx=idx, ki=ki, h=h, acc=acc, qi=qi: e.matmul(
                                          acc[c][:, 0:129], lhsT=ptc[c][:, idx * 128:(idx + 1) * 128], rhs=Vx[:, ki, h, :],
                                          start=(ki == 0), stop=(ki == qi)),
                                          reads=[ptc[c], Vx], writes=[acc[c]] if first else [], chain=[] if first else [acc[c]])
                          sc.op("dve", lambda e, acc=acc: e.reciprocal(rz[:, 0:1], acc[0][:, 128:129]), reads=[acc[0]], writes=[rz])
                          sc.op("dve", lambda e, acc=acc: e.reciprocal(rz[:, 1:2], acc[1][:, 128:129]), reads=[acc[1], rz], writes=[rz])
                          sc.op("dve", lambda e: e.tensor_tensor(rz[:, 1:2], rz[:, 1:2], lams[:, 3:4], ALU.mult), reads=[rz, lams], writes=[rz])
                          sc.op("act", lambda e, acc=acc, ot=ot, h=h: e.activation(ot[:, h, :], acc[0][:, 0:128], AF.Copy, scale=rz[:, 0:1]),
                                reads=[acc[0], rz], writes=[ot])
                          sc.op("dve", lambda e, acc=acc, ot=ot, h=h: e.scalar_tensor_tensor(
                              ot[:, h, :], acc[1][:, 0:128], rz[:, 1:2], ot[:, h, :], ALU.mult, ALU.add),
                              reads=[acc[1], rz, ot], writes=[ot])
                      sc.dma("sp", oa_sc[G(q0):G(q0) + 128, :], ot[:].rearrange("p h e -> p (h e)"), [ot], [oa_sc], ot)
                  if has_sample:
                      ot = oat[NT % 2]
                      sc.op("pool", lambda e, ot=ot: e.memset(ot[:], 0.0), writes=[ot])
                      sc.dma("sp", oa_sc[G(S):G(S) + 128, :], ot[:].rearrange("p h e -> p (h e)"), [ot], [oa_sc], ot)
                  sc.barrier()
                  sc.end_scope(e1)

            if has_sample and cfg.phaseS:
              with ExitStack() as e6:
                sb6 = lambda n, sh, dt=F32: sc.sbuf(n, sh, dt, e6)
                NPG = NSQ * NPP
                pti = sb6("pti", [128, NPG], I32); ptf = sb6("ptf", [128, NPG]); idx = sb6("idx", [128, NPG], I32)
                wqs = sb6("wqs", [128, 8, 128]); wq = sb6("wq", [128, 8, 1024], BF16)
                bq2 = sb6("bq2", [64, 16])
                Qs = [sb6(f"Qs{c}", [64, 8, 128]) for c in range(2)]
                kp = [sb6(f"kp{i}", [128, 1024]) for i in range(2)]
                vp = [sb6(f"vp{i}", [128, 1024]) for i in range(2)]
                kT = [sb6(f"kTs{i}", [64, 4, 128]) for i in range(2)]
                PTs = [sb6(f"PTs{i}", [128, 64]) for i in range(2)]
                knj = sb6("knj", [4, 1024]); vnj = sb6("vnj", [4, 1024])
                kTn = sb6("kTn", [64, 16, 4]); sn = sb6("sn", [4, 64]); PTn = sb6("PTn", [4, 64])
                zer = sb6("zer", [128, 512]); ones2 = sb6("ones2", [128, 2])
                rzs = sb6("rzs", [4, 32]); oaj = sb6("oaj", [4, 8, 128])
                psT = ps[0:2]; psSb = ps[2]; psOs = ps[3:7]; psZ = ps[7]
                sc.op("pool", lambda e: e.memset(zer[:], 0.0), writes=[zer])
                sc.op("pool", lambda e: e.memset(ones2[:], 1.0), writes=[ones2])
                sc.dma("sp", pti[:], ptab_d[0:1, :].partition_broadcast(128), [ptab_d], [pti], pti)
                sc.op("dve", lambda e: e.tensor_copy(ptf[:], pti[:]), reads=[pti], writes=[ptf])
                sc.op("dve", lambda e: e.tensor_scalar(ptf[:], ptf[:], 128.0, smallc[:, 88:89], ALU.mult, ALU.add), reads=[ptf, smallc], writes=[ptf])
                sc.op("dve", lambda e: e.tensor_copy(idx[:], ptf[:]), reads=[ptf], writes=[idx])
                for blk in range(8):
                    sc.dma("sp", wqs[:], w_in[:, blk * 128:(blk + 1) * 128].rearrange("(kc p) c -> p kc c", p=128), [w_in], [wqs], wqs)
                    sc.op("pool", lambda e, blk=blk: e.tensor_copy(wq[:, :, blk * 128:(blk + 1) * 128], wqs[:]), reads=[wqs], writes=[wq])
                sc.op("dve", lambda e: e.tensor_scalar(bq2[:], smallc[0:64, 72:88], 0.125, None, ALU.mult), reads=[smallc], writes=[bq2])
                for h in range(8):
                    for c in range(2):
                        pb = nps2()
                        mm_chain(pb, pb[0:64, 0:128], [(wq[:, kc, h * 128 + c * 64:h * 128 + c * 64 + 64], xT[:, kc, S:S + 128]) for kc in range(8)], [wq, xT])
                        sc.op("act", lambda e, pb=pb, h=h, c=c: e.activation(
                            Qs[c][:, h, :], pb[0:64, 0:128], AF.Identity, bias=bq2[:, h * 2 + c:h * 2 + c + 1], scale=0.125),
                            reads=[pb, bq2], writes=[Qs[c]])
                pgi = 0
                for j in range(NSQ):
                    qs = slice(4 * j, 4 * j + 4)
                    for bk in list(psOs) + [psZ]:
                        sc.op("pe", lambda e, bk=bk: e.matmul(bk[0:4, :], lhsT=zer[:, 0:4], rhs=zer[:, :], start=True, stop=True),
                              reads=[zer], writes=[bk])

                    def accum(ptile, nk, vtile):
                        for h in range(8):
                            for c in range(2):
                                hc = h * 2 + c
                                bk = psOs[hc // 4]
                                sc.op("pe", lambda e, bk=bk, hc=hc, h=h: e.matmul(
                                    bk[0:4, (hc % 4) * 128:(hc % 4 + 1) * 128], lhsT=ptile[0:nk, hc * 4:hc * 4 + 4], rhs=vtile[0:nk, h * 128:(h + 1) * 128],
                                    start=False, stop=False, skip_group_check=True), reads=[ptile, vtile], chain=[bk])
                                sc.op("pe", lambda e, hc=hc: e.matmul(
                                    psZ[0:4, hc * 2:hc * 2 + 2], lhsT=ptile[0:nk, hc * 4:hc * 4 + 4], rhs=ones2[0:nk, :],
                                    start=False, stop=False, skip_group_check=True), reads=[ptile, ones2], chain=[psZ])
                    for p in range(NPP):
                        kpt, vpt = kp[pgi % 2], vp[pgi % 2]
                        col = j * NPP + p
                        sc.dma("pool", kpt[:], ck_d[:, :], [ck_d, idx], [kpt], kpt,
                               indirect=dict(out_offset=None, in_offset=bass.IndirectOffsetOnAxis(ap=idx[:, col:col + 1], axis=0)))
                        sc.dma("pool", vpt[:], cv_d[:, :], [cv_d, idx], [vpt], vpt,
                               indirect=dict(out_offset=None, in_offset=bass.IndirectOffsetOnAxis(ap=idx[:, col:col + 1], axis=0)))
                        first_s = True
                        for hq in range(4):
                            pt_ = psT[hq % 2]
                            for hl in range(2):
                                for c in range(2):
                                    h = hq * 2 + hl
                                    o0 = (hl * 2 + c) * 128
                                    f0 = (hl == 0 and c == 0)
                                    sc.op("pe", lambda e, pt_=pt_, o0=o0, h=h, c=c, kpt=kpt: e.transpose(
                                        pt_[0:64, o0:o0 + 128], kpt[:, h * 128 + c * 64:h * 128 + c * 64 + 64], ident),
                                        reads=[kpt, cst], writes=[pt_] if f0 else [], chain=[] if f0 else [pt_])
                            kTt = kT[hq % 2]
                            if hq % 2 == 0:
                                sc.op("act", lambda e, pt_=pt_, kTt=kTt: e.copy(kTt[:].rearrange("p a k -> p (a k)"), pt_[0:64, :]), reads=[pt_], writes=[kTt])
                            else:
                                sc.op("dve", lambda e, pt_=pt_, kTt=kTt: e.tensor_copy(kTt[:].rearrange("p a k -> p (a k)"), pt_[0:64, :]), reads=[pt_], writes=[kTt])
                            for hl in range(2):
                                for c in range(2):
                                    h = hq * 2 + hl
                                    hc = h * 2 + c
                                    sc.op("pe", lambda e, kTt=kTt, hl=hl, c=c, h=h, hc=hc: e.matmul(
                                        psSb[:, hc * 4:hc * 4 + 4], lhsT=kTt[:, hl * 2 + c, :], rhs=Qs[c][:, h, qs], start=True, stop=True),
                                        reads=[kTt, Qs[c]], writes=[psSb] if first_s else [], chain=[] if first_s else [psSb])
                                    first_s = False
                        ptile = PTs[pgi % 2]
                        sc.op("act", lambda e, ptile=ptile: e.activation(ptile[:], psSb[:, 0:64], AF.Exp), reads=[psSb], writes=[ptile])
                        accum(ptile, 128, vpt)
                        pgi += 1
                    r0 = G(S) + 4 * j
                    sc.dma("sp", knj[:], k_out[r0:r0 + 4, :], [k_out], [knj], knj)
                    sc.dma("sp", vnj[:], v_out[r0:r0 + 4, :], [v_out], [vnj], vnj)
                    pt_ = psT[0]
                    for hc in range(16):
                        sc.op("pe", lambda e, hc=hc, pt_=pt_: e.transpose(pt_[0:64, hc * 4:hc * 4 + 4], knj[0:4, hc * 64:hc * 64 + 64], ident[0:4, 0:4]),
                              reads=[knj, cst], writes=[pt_] if hc == 0 else [], chain=[] if hc == 0 else [pt_])
                    sc.op("act", lambda e, pt_=pt_: e.copy(kTn[:].rearrange("p a k -> p (a k)"), pt_[0:64, 0:64]), reads=[pt_], writes=[kTn])
                    for hc in range(16):
                        h, c = hc // 2, hc % 2
                        sc.op("pe", lambda e, hc=hc, h=h, c=c: e.matmul(
                            psSb[0:4, hc * 4:hc * 4 + 4], lhsT=kTn[:, hc, :], rhs=Qs[c][:, h, qs], start=True, stop=True),
                            reads=[kTn, Qs[c]], writes=[psSb] if hc == 0 else [], chain=[] if hc == 0 else [psSb])
                    sc.op("dve", lambda e: e.tensor_tensor(sn[:], psSb[0:4, 0:64], cst[0:4, 26, 0:64], ALU.add), reads=[psSb, cst], writes=[sn])
                    sc.op("act", lambda e: e.activation(PTn[:], sn[:], AF.Exp), reads=[sn], writes=[PTn])
                    accum(PTn, 4, vnj)
                    sc.op("dve", lambda e: e.reciprocal(rzs[:], psZ[0:4, 0:32]), reads=[psZ], writes=[rzs])
                    for h in range(8):
                        c1 = 2 * (2 * h + 1)
                        sc.op("dve", lambda e, c1=c1: e.tensor_tensor(rzs[:, c1:c1 + 1], rzs[:, c1:c1 + 1], lams[0:4, 3:4], ALU.mult),
                              reads=[rzs, lams], writes=[rzs])
                    for h in range(8):
                        b0, o0 = psOs[(2 * h) // 4], ((2 * h) % 4) * 128
                        b1, o1 = psOs[(2 * h + 1) // 4], ((2 * h + 1) % 4) * 128
                        sc.op("act", lambda e, h=h, b0=b0, o0=o0: e.activation(oaj[:, h, :], b0[0:4, o0:o0 + 128], AF.Copy, scale=rzs[:, 4 * h:4 * h + 1]),
                              reads=[b0, rzs], writes=[oaj])
                        sc.op("dve", lambda e, h=h, b1=b1, o1=o1: e.scalar_tensor_tensor(
                            oaj[:, h, :], b1[0:4, o1:o1 + 128], rzs[:, 4 * h + 2:4 * h + 3], oaj[:, h, :], ALU.mult, ALU.add),
                            reads=[b1, rzs, oaj], writes=[oaj])
                    sc.dma("sp", oa_sc[r0:r0 + 4, :], oaj[:].rearrange("p h e -> p (h e)"), [oaj], [oa_sc], oaj)
                sc.barrier()
                sc.end_scope(e6)

            with ExitStack() as e2:
              if not cfg.peer_only:
                  sb = lambda n, sh, dt=F32: sc.sbuf(n, sh, dt, e2)
                  wu = sb("wu", [128, 8, 1024], BF16)
                  wvm = sb("wvm", [128, 8, 1024], BF16)
                  wg = sb("wg", [128, 8, 8], BF16)
                  wgs = sb("wgs", [128, 8, 8])
                  wqm = sb("wqm", [128, 4, 2, 256])
                  wkm = sb("wkm", [128, 4, 2, 256])
                  bvm = sb("bvm", [128, 1024], BF16)
                  bg = sb("bg", [128, 8])
                  mst = sb("mst_sb", [128, 4 + 4 * NSQ])
                  mstp = sb("mstp", [128, 4])
                  CT = [[sb(f"CT{j}_{h}", [128, 2, 257]) for h in range(4)] for j in range(1 + (NSQ if has_sample else 0))]
                  uTe = sb("uTe", [128, 8, 131])
                  uTsV = uTe[:, :, 0:NSQ * 7].rearrange("p c (s t) -> p c s t", t=7)
                  ucT = sb("ucT", [128, 8, 128])
                  qT = sb("qT", [128, 4, 2, 128])
                  kT = sb("kT", [128, 4, 2, 128])
                  ktm = sb("ktm", [128, 4, 256])
                  vext = sb("vext", [128, 4, 257])
                  g = sb("g", [128, 8])
                  t1 = sb("t1", [128, 4]); t2 = sb("t2", [128, 4])
                  bmt = sb("bmt", [128, 8])
                  imb = sb("imb", [128, 4])
                  Dm = sb("Dm", [128, 4, 128])
                  swT = sb("swT", [128, 4, 128])
                  diag = swT
                  rmax = sb("rmax", [128, 4]); bm = sb("bm", [128, 4]); negm = sb("negm", [128, 4])
                  inter = sb("inter", [128, 4]); eneg = sb("eneg", [128, 4]); dlt = sb("dlt", [128, 4])
                  Bs = sb("Bs", [128, 257]); nd = sb("nd", [128, 257])
                  den = sb("den", [128, 1]); rden = sb("rden", [128, 1])
                  hh = sb("hh", [128, 4, 256])
                  ctoV = Dm[:].rearrange("p (a b) s -> p a (b s)", a=2)
                  qTmV = Bs[:, 0:256].rearrange("p (d t) -> p d t", d=2)
                  wkV = Bs[:, 0:256]
                  lr = sb("lr", [128, 8]); lj = sb("lj", [128, 8])
                  wv = sb("wv", [128, 4]); wj = sb("wj", [128, 4]); tmp4 = sb("tmp4", [128, 4]); dec = sb("dec", [128, 4])

                  for (dstw, c00) in ((wu, 3072), (wvm, 4096)):
                      for blk in range(8):
                          wstV = hh[:].rearrange("p h (a b) -> p (h a) b", a=2)
                          sc.dma("sp", wstV, w_in[:, c00 + blk * 128:c00 + (blk + 1) * 128].rearrange("(kc p) c -> p kc c", p=128),
                                 [w_in], [hh], hh)
                          sc.op("pool", lambda e, blk=blk, dstw=dstw, wstV=wstV: e.tensor_copy(dstw[:, :, blk * 128:(blk + 1) * 128], wstV),
                                reads=[hh], writes=[dstw])
                  sc.dma("sp", wgs[:], w_in[:, 6144:6152].rearrange("(kc p) c -> p kc c", p=128), [w_in], [wgs], wgs)
                  sc.op("pool", lambda e: e.tensor_copy(wg[:], wgs[:]), reads=[wgs], writes=[wg])
                  sc.dma("sp", wqm[:], wqm_d[:].rearrange("h (dc p) e -> p h dc e", p=128), [wqm_d], [wqm], wqm)
                  sc.dma("sp", wkm[:], wkm_d[:].rearrange("h (dc p) e -> p h dc e", p=128), [wkm_d], [wkm], wkm)
                  sc.dma("sp", ucT[:].rearrange("p c t -> p (c t)"), b_in[0:1, 4096:5120].partition_broadcast(128), [b_in], [ucT], ucT)
                  sc.op("pool", lambda e: e.tensor_copy(bvm[:], ucT[:].rearrange("p c t -> p (c t)")), reads=[ucT], writes=[bvm])
                  sc.dma("sp", bg[:], b_in[0:1, 6144:6152].partition_broadcast(128), [b_in], [bg], bg)
                  sc.dma("sp", mst[:], mst_d[:], [mst_d], [mst], mst)
                  buT = smallc[:, 0:8]
                  wcT = smallc[:, 8:40].rearrange("p (c j) -> p c j", j=4)
                  bcT = smallc[:, 40:48]
                  rowmask = smallc[:, 48:56]
                  for h in range(4):
                      sc.op("pool", lambda e, h=h: e.memset(CT[0][h][:], 0.0), writes=[CT[0][h]])
                  sc.op("pool", lambda e: e.memset(uTe[:], 0.0), writes=[uTe])
                  sc.op("pool", lambda e: e.memset(mstp[:], 0.0), writes=[mstp])
                  sc.op("pool", lambda e: e.memset(ucT[:], 0.0), writes=[ucT])
                  sc.op("pool", lambda e: e.memset(vext[:], 1.0), writes=[vext])
                  for j in range(NSQ if has_sample else 0):
                      for h in range(4):
                          sc.dma("sp", ctoV, stC_d[j, h].rearrange("(ec p) d -> p ec d", p=128), [stC_d], [Dm], Dm)
                          pb = nps()
                          for ec in range(2):
                              for dc in range(2):
                                  first = (ec == 0 and dc == 0)
                                  sc.op("pe", lambda e, pb=pb, ec=ec, dc=dc: e.transpose(
                                      pb[:, (dc * 2 + ec) * 128:(dc * 2 + ec + 1) * 128], ctoV[:, ec, dc * 128:(dc + 1) * 128], ident),
                                      reads=[Dm, cst], writes=[pb] if first else [], chain=[] if first else [pb])
                          sc.op("act", lambda e, pb=pb, j=j, h=h: e.copy(
                              CT[j + 1][h][:, :, 0:256], pb[:, :].rearrange("p (dc e) -> p dc e", dc=2)),
                              reads=[pb], writes=[CT[j + 1][h]])
                          sc.dma("sp", CT[j + 1][h][:, :, 256:257], stn_d[j, h].rearrange("(dc p o) -> p dc o", p=128, o=1),
                                 [stn_d], [CT[j + 1][h]], CT[j + 1][h], slow=True)

                  for ti in range(NTT):
                      t0 = tok0[ti]
                      sample = (ti == NT)
                      if sample:
                          for c_ in range(8):
                              sc.dma("sp", uTsV[:, c_, :, 0:3], convT_d[:, c_], [convT_d], [uTe], uTe, slow=True)
                      xs_tok = [xT[:, kc, t0:t0 + 128] for kc in range(8)]
                      for half in range(2):
                          pb = nps()
                          for c4 in range(4):
                              c = half * 4 + c4
                              mm_chain(pb, pb[:, c4 * 128:(c4 + 1) * 128],
                                       [(wu[:, kc, c * 128:(c + 1) * 128], xs_tok[kc]) for kc in range(8)], [wu, xT],
                                       first_group=(c4 == 0))
                          for c4 in range(4):
                              c = half * 4 + c4
                              if not sample:
                                  sc.op("act", lambda e, pb=pb, c=c, c4=c4: e.activation(
                                      uTe[:, c, 3:131], pb[:, c4 * 128:(c4 + 1) * 128], AF.Identity, bias=buT[:, c:c + 1]),
                                      reads=[pb, smallc], writes=[uTe])
                              else:
                                  sc.op("act", lambda e, pb=pb, c=c, c4=c4: e.activation(
                                      uTsV[:, c, :, 3:7], pb[:, c4 * 128:c4 * 128 + 4 * NSQ].rearrange("p (s t) -> p s t", t=4),
                                      AF.Identity, bias=buT[:, c:c + 1]),
                                      reads=[pb, smallc], writes=[uTe])
                      for c in range(8):
                          if not sample:
                              src = lambda j, c=c: uTe[:, c, j:j + 128]
                              dst = ucT[:, c, :]
                              rd = [uTe, smallc]
                          else:
                              src = lambda j, c=c: uTsV[:, c, :, j:j + 4]
                              dst = ucT[:, c, 0:4 * NSQ].rearrange("p (s t) -> p s t", t=4)
                              rd = [uTe, smallc]
                          sc.op("dve", lambda e, src=src, dst=dst, c=c: e.tensor_scalar(
                              dst, src(0), wcT[:, c, 0:1], bcT[:, c:c + 1], ALU.mult, ALU.add), reads=rd, writes=[ucT])
                          for j in range(1, 4):
                              sc.op("dve", lambda e, src=src, dst=dst, c=c, j=j: e.scalar_tensor_tensor(
                                  dst, src(j), wcT[:, c, j:j + 1], dst, ALU.mult, ALU.add), reads=rd + [ucT], writes=[ucT])
                      sc.op("act", lambda e: e.activation(ucT[:], ucT[:], AF.Silu), reads=[ucT], writes=[ucT])
                      if not sample:
                          sc.op("dve", lambda e: e.tensor_copy(uTe[:, :, 0:3], uTe[:, :, 128:131]), reads=[uTe], writes=[uTe])
                      for (wsrc, dstT, scale) in ((wqm, qT, 1.0), (wkm, kT, 1.0 / 16.0)):
                          for hp in range(2):
                              pb = nps()
                              for hh_ in range(2):
                                  h = hp * 2 + hh_
                                  for ec in range(2):
                                      o0 = (hh_ * 2 + ec) * 128
                                      mm_chain(pb, pb[:, o0:o0 + 128],
                                               [(wsrc[:, h, dc, ec * 128:(ec + 1) * 128], ucT[:, h * 2 + dc, :]) for dc in range(2)],
                                               [wsrc, ucT], first_group=(hh_ == 0 and ec == 0))
                              sc.op("act", lambda e, pb=pb, hp=hp, dstT=dstT, scale=scale: e.activation(
                                  dstT[:, hp * 2:hp * 2 + 2, :, :], pb[:, :].rearrange("p (h ec t) -> p h ec t", h=2, ec=2),
                                  AF.Copy, scale=scale), reads=[pb], writes=[dstT])
                      for hp in range(2):
                          pb = nps()
                          for hh_ in range(2):
                              h = hp * 2 + hh_
                              mm_chain(pb, pb[:, hh_ * 256:(hh_ + 1) * 256],
                                       [(ucT[:, h * 2 + dc, :], wkm[:, h, dc, :]) for dc in range(2)], [wkm, ucT],
                                       first_group=(hh_ == 0))
                          sc.op("act", lambda e, pb=pb, hp=hp: e.activation(
                              ktm[:, hp * 2:hp * 2 + 2, :], pb[:, :].rearrange("p (h e) -> p h e", h=2), AF.Copy, scale=1.0 / 16.0),
                              reads=[pb], writes=[ktm])
                      for blk in range(2):
                          pb = nps()
                          mm_chain(pb, pb[:, :], [(xs_tok[kc], wvm[:, kc, blk * 512:(blk + 1) * 512]) for kc in range(8)], [xT, wvm])
                          sc.op("dve", lambda e, pb=pb, blk=blk: e.tensor_tensor(
                              vext[:, blk * 2:blk * 2 + 2, 0:256], pb[:, :].rearrange("p (h e) -> p h e", h=2),
                              bvm[:, blk * 512:(blk + 1) * 512].rearrange("p (h e) -> p h e", h=2), ALU.add),
                              reads=[pb, bvm], writes=[vext])
                      pb = nps()
                      mm_chain(pb, pb[:, 0:8], [(xs_tok[kc], wg[:, kc, :]) for kc in range(8)], [xT, wg])
                      sc.op("dve", lambda e, pb=pb: e.tensor_tensor(g[:], pb[:, 0:8], bg[:], ALU.add), reads=[pb, bg], writes=[g])
                      tri = cst[:, 3 if sample else 2, :]
                      mneg = cst[:, 5 if sample else 4, :]
                      selrow = cst[:, 15 if sample else 6, :]
                      mstb = mst if sample else mstp
                      mst_cur = mstb[:, 0:4]
                      sc.op("act", lambda e: e.activation(t1[:], g[:, 4:8], AF.Exp, scale=-1.0), reads=[g], writes=[t1])
                      sc.op("act", lambda e: e.activation(t2[:], t1[:], AF.Ln, bias=1.0), reads=[t1], writes=[t2])
                      pb = nps()
                      mm_chain(pb, pb[:, 0:4], [(tri, t2[:])], [cst, t2])
                      sc.op("dve", lambda e, pb=pb: e.tensor_scalar(bmt[:, 0:4], pb[:, 0:4], -1.0, None, ALU.mult),
                            reads=[pb], writes=[bmt])
                      sc.op("dve", lambda e: e.tensor_tensor(imb[:], g[:, 0:4], bmt[:, 0:4], ALU.subtract), reads=[g, bmt], writes=[imb])
                      for h in range(4):
                          sc.op("dve", lambda e, h=h: e.tensor_scalar(diag[:, h, :], ident, imb[:, h:h + 1], None, ALU.mult),
                                reads=[cst, imb], writes=[diag])
                      pb = nps()
                      mm_chain(pb, pb[:, :], [(ones, diag[:].rearrange("p h s -> p (h s)"))], [cst, diag])
                      for h in range(4):
                          sc.op("dve", lambda e, pb=pb, h=h: e.scalar_tensor_tensor(
                              Dm[:, h, :], pb[:, h * 128:(h + 1) * 128], bmt[:, h:h + 1], mneg, ALU.add, ALU.add),
                              reads=[pb, bmt, cst], writes=[Dm])
                      sc.op("dve", lambda e: e.tensor_reduce(rmax[:], Dm[:], AX.X, ALU.max), reads=[Dm], writes=[rmax])
                      sc.op("dve", lambda e: e.tensor_tensor(bm[:], bmt[:, 0:4], mst_cur, ALU.add), reads=[bmt, mstb], writes=[bm])
                      sc.op("dve", lambda e: e.tensor_tensor(bmt[:, 4:8], bm[:], rmax[:], ALU.max), reads=[bm, rmax, bmt], writes=[bmt])
                      sc.op("dve", lambda e: e.tensor_scalar(negm[:], bmt[:, 4:8], -1.0, None, ALU.mult), reads=[bmt], writes=[negm])
                      for h in range(4):
                          sc.op("act", lambda e, h=h: e.activation(Dm[:, h, :], Dm[:, h, :], AF.Exp, bias=negm[:, h:h + 1]),
                                reads=[Dm, negm], writes=[Dm])
                      sc.op("dve", lambda e: e.tensor_tensor(dlt[:], bm[:], bmt[:, 4:8], ALU.subtract), reads=[bm, bmt], writes=[dlt])
                      sc.op("act", lambda e: e.activation(inter[:], dlt[:], AF.Exp), reads=[dlt], writes=[inter])
                      sc.op("act", lambda e: e.activation(eneg[:], bmt[:, 4:8], AF.Exp, scale=-1.0), reads=[bmt], writes=[eneg])
                      pb = nps()
                      for h in range(4):
                          mm_chain(pb, pb[:, h * 128:(h + 1) * 128], [(qT[:, h, dc, :], kT[:, h, dc, :]) for dc in range(2)],
                                   [qT, kT], first_group=(h == 0))
                      sc.op("dve", lambda e, pb=pb: e.tensor_tensor(Dm[:].rearrange("p h s -> p (h s)"), pb[:, :],
                                                                   Dm[:].rearrange("p h s -> p (h s)"), ALU.mult),
                            reads=[pb, Dm], writes=[Dm])
                      pb = nps()
                      for h in range(4):
                          sc.op("pe", lambda e, pb=pb, h=h: e.transpose(pb[:, h * 128:(h + 1) * 128], Dm[:, h, :], ident),
                                reads=[Dm, cst], writes=[pb] if h == 0 else [], chain=[] if h == 0 else [pb])
                      sc.op("act", lambda e, pb=pb: e.copy(swT[:].rearrange("p h s -> p (h s)"), pb[:, :]), reads=[pb], writes=[swT])
                      seqs = [0] if not sample else list(range(1, NSQ + 1))
                      for h in range(4):
                          pa = nps()
                          pairs = []
                          if not sample:
                              pairs = [(qT[:, h, dc, :], CT[0][h][:, dc, :]) for dc in range(2)]
                              rds = [qT, CT[0][h]]
                          else:
                              nmm = 2 * len(seqs)
                              im = 0
                              for j in seqs:
                                  for dc in range(2):
                                      sc.op("dve", lambda e, h=h, dc=dc, j=j: e.tensor_tensor(
                                          qTmV[:, dc, :], qT[:, h, dc, :], cst[:, 15 + j, :], ALU.mult),
                                          reads=[qT, cst], writes=[Bs])
                                  for dc in range(2):
                                      sc.op("pe", lambda e, pa=pa, dc=dc, j=j, h=h, im=im: e.matmul(
                                          pa[:, 0:257], lhsT=qTmV[:, dc, :], rhs=CT[j][h][:, dc, :],
                                          start=(im == 0), stop=(im == nmm - 1)),
                                          reads=[Bs, CT[j][h]], writes=[pa] if im == 0 else [], chain=[] if im == 0 else [pa])
                                      im += 1
                          if not sample:
                              mm_chain(pa, pa[:, 0:257], pairs, rds)
                          pb2 = nps()
                          mm_chain(pb2, pb2[:, 0:257], [(swT[:, h, :], vext[:, h, :])], [swT, vext])
                          sc.op("act", lambda e, pb2=pb2: e.copy(Bs[:], pb2[:, 0:257]), reads=[pb2], writes=[Bs])
                          sc.op("dve", lambda e, pa=pa, h=h: e.scalar_tensor_tensor(
                              nd[:], pa[:, 0:257], inter[:, h:h + 1], Bs[:], ALU.mult, ALU.add), reads=[pa, inter, Bs], writes=[nd])
                          sc.op("dve", lambda e: e.tensor_scalar(rden[:], nd[:, 256:257], -1.0, None, ALU.mult), reads=[nd], writes=[rden])
                          sc.op("dve", lambda e: e.tensor_tensor(den[:], nd[:, 256:257], rden[:], ALU.max), reads=[nd, rden], writes=[den])
                          sc.op("dve", lambda e, h=h: e.tensor_tensor(den[:], den[:], eneg[:, h:h + 1], ALU.max), reads=[den, eneg], writes=[den])
                          sc.op("dve", lambda e: e.reciprocal(rden[:], den[:]), reads=[den], writes=[rden])
                          sc.op("dve", lambda e, h=h: e.tensor_scalar(hh[:, h, :], nd[:, 0:256], rden[:, 0:1], None, ALU.mult),
                                reads=[nd, rden], writes=[hh])
                      sc.dma("sp", h_out[G(t0):G(t0) + 128, :], hh[:].rearrange("p h e -> p (h e)"), [hh], [h_out], hh, is_output=True)
                      pb = nps()
                      mm_chain(pb, pb[:, 0:8], [(selrow, bmt[:])], [cst, bmt])
                      sc.op("act", lambda e, pb=pb: e.copy(lr[:], pb[:, 0:8]), reads=[pb], writes=[lr])
                      sc.op("dve", lambda e: e.tensor_tensor(tmp4[:], lr[:, 0:4], lr[:, 4:8], ALU.subtract), reads=[lr], writes=[tmp4])
                      sc.op("dve", lambda e: e.tensor_tensor(tmp4[:], tmp4[:], imb[:], ALU.add), reads=[tmp4, imb], writes=[tmp4])
                      sc.op("act", lambda e: e.activation(wv[:], tmp4[:], AF.Exp), reads=[tmp4], writes=[wv])
                      for j in seqs:
                          if not sample:
                              ljt = lr
                              mrep = mstp[:, 0:4]
                          else:
                              pbj = nps()
                              mm_chain(pbj, pbj[:, 0:8], [(cst[:, 6 + j, :], bmt[:])], [cst, bmt])
                              sc.op("act", lambda e, pbj=pbj: e.copy(lj[:], pbj[:, 0:8]), reads=[pbj], writes=[lj])
                              ljt = lj
                              mrep = mst[:, 4 * j:4 * j + 4]
                          sc.op("dve", lambda e, ljt=ljt, mrep=mrep: e.tensor_tensor(dec[:], ljt[:, 0:4], mrep, ALU.add),
                                reads=[ljt, mstb], writes=[dec])
                          sc.op("dve", lambda e, ljt=ljt: e.tensor_tensor(dec[:], dec[:], ljt[:, 4:8], ALU.subtract),
                                reads=[ljt, dec], writes=[dec])
                          sc.op("act", lambda e: e.activation(dec[:], dec[:], AF.Exp), reads=[dec], writes=[dec])
                          if not sample:
                              wjt = wv
                          else:
                              sc.op("dve", lambda e, j=j: e.tensor_scalar(wj[:], wv[:], rowmask[:, j - 1:j], None, ALU.mult),
                                    reads=[wv, smallc], writes=[wj])
                              wjt = wj
                          for h in range(4):
                              sc.op("dve", lambda e, h=h, wjt=wjt: e.tensor_scalar(wkV, ktm[:, h, :], wjt[:, h:h + 1], None, ALU.mult),
                                    reads=[ktm, wjt], writes=[Bs])
                              for dc in range(2):
                                  pu = nps()
                                  mm_chain(pu, pu[:, 0:257], [(wkV[:, dc * 128:(dc + 1) * 128], vext[:, h, :])], [Bs, vext])
                                  sc.op("dve", lambda e, pu=pu, j=j, h=h, dc=dc: e.scalar_tensor_tensor(
                                      CT[j][h][:, dc, :], CT[j][h][:, dc, :], dec[:, h:h + 1], pu[:, 0:257], ALU.mult, ALU.add),
                                      reads=[pu, dec, CT[j][h]], writes=[CT[j][h]])
                          sc.op("dve", lambda e, ljt=ljt, mrep=mrep: e.tensor_copy(mrep, ljt[:, 4:8]), reads=[ljt], writes=[mstb])

                  for j in range(1 + (NSQ if has_sample else 0)):
                      oj = jb if j == 0 else NPS + j - 1
                      for h in range(4):
                          pb = nps()
                          for dc in range(2):
                              for ec in range(2):
                                  first = (ec == 0 and dc == 0)
                                  sc.op("pe", lambda e, pb=pb, ec=ec, dc=dc, j=j, h=h: e.transpose(
                                      pb[:, (ec * 2 + dc) * 128:(ec * 2 + dc + 1) * 128], CT[j][h][:, dc, ec * 128:(ec + 1) * 128], ident),
                                      reads=[CT[j][h], cst], writes=[pb] if first else [], chain=[] if first else [pb])
                          sc.op("act", lambda e, pb=pb: e.copy(ctoV, pb[:, :].rearrange("p (ec d) -> p ec d", ec=2)),
                                reads=[pb], writes=[Dm])
                          sc.dma("sp", C_out[oj, h].rearrange("(ec p) d -> p ec d", p=128), ctoV, [Dm], [C_out], Dm, is_output=True)
                          sc.dma("sp", n_out[oj, h].rearrange("(dc p o) -> p dc o", p=128, o=1), CT[j][h][:, :, 256:257],
                                 [CT[j][h]], [n_out], CT[j][h], is_output=True, slow=True)
                      msrc = mstp[0:1, 0:4] if j == 0 else mst[0:1, 4 * j:4 * j + 4]
                      mb_ = mstp if j == 0 else mst
                      sc.dma("sp", m_out[oj:oj + 1, :], msrc, [mb_], [m_out], mb_, is_output=True)
                  sc.barrier()
                  sc.end_scope(e2)
            def load_w2(stg, dst, dram, c0, ncols=1024):
                for blk in range(ncols // 512):
                    sc.dma("sp", stg[:], dram[:, c0 + blk * 512:c0 + (blk + 1) * 512].rearrange("(kc p) c -> p kc c", p=128),
                           [dram], [stg], stg)
                    sc.op("pool", lambda e, blk=blk: e.tensor_copy(dst[:, :, blk * 512:(blk + 1) * 512], stg[:]),
                          reads=[stg], writes=[dst])

            def transpose_tm(src_tile, dstT, rd):
                for half in range(2):
                    pb = nps()
                    for j in range(4):
                        kc = half * 4 + j
                        sc.op("pe", lambda e, pb=pb, j=j, kc=kc: e.transpose(pb[:, j * 128:(j + 1) * 128], src_tile(kc), ident),
                              reads=[rd, cst], writes=[pb] if j == 0 else [], chain=[] if j == 0 else [pb])
                    src = pb[:, :].rearrange("p (j r) -> p j r", j=4)
                    if half == 0:
                        sc.op("act", lambda e, src=src: e.copy(dstT[:, 0:4, :], src), reads=[pb], writes=[dstT])
                    else:
                        sc.op("dve", lambda e, src=src: e.tensor_copy(dstT[:, 4:8, :], src), reads=[pb], writes=[dstT])

            if cfg.phaseC and not cfg.peer_only:
              with ExitStack() as e3:
                sb3 = lambda n, sh, dt=F32: sc.sbuf(n, sh, dt, e3)
                wstC = sb3("wstC", [128, 8, 512])
                w_ga = sb3("w_ga", [128, 8, 1024], BF16)
                w_a_sb = sb3("w_a_sb", [128, 8, 1024], BF16)
                b_ga = sb3("b_ga", [128, 1024])
                sublg = sb3("sublg", [128, 128])
                oat3 = sb3("oat3", [128, 8, 128]); sq = sb3("sq", [128, 8, 128])
                ms = sb3("ms", [128, 8]); rstd = sb3("rstd", [128, 8])
                oan = sb3("oan", [128, 8, 128]); oanT = sb3("oanT", [128, 8, 128], BF16)
                sg = sb3("sg", [128, 512]); mat = sb3("mat", [128, 1024])
                load_w2(wstC, w_ga, w_in, 6152)
                load_w2(wstC, w_a_sb, wa_d, 0)
                sc.dma("sp", b_ga[:], b_in[0:1, 6152:7176].partition_broadcast(128), [b_in], [b_ga], b_ga)
                sc.dma("sp", sublg[:], subg_d[0:1, :].partition_broadcast(128), [subg_d], [sublg], sublg)
                sc.op("dve", lambda e: e.tensor_scalar(sublg[:], sublg[:], 1.0 - LAM_INIT, None, ALU.mult), reads=[sublg], writes=[sublg])
                for ti in range(NTT):
                    t0 = tok0[ti]
                    sc.dma("sp", oat3[:].rearrange("p h e -> p (h e)"), oa_sc[G(t0):G(t0) + 128, :], [oa_sc], [oat3], oat3)
                    sc.op("dve", lambda e: e.tensor_tensor(sq[:], oat3[:], oat3[:], ALU.mult), reads=[oat3], writes=[sq])
                    sc.op("dve", lambda e: e.tensor_reduce(ms[:], sq[:], AX.X, ALU.add), reads=[sq], writes=[ms])
                    sc.op("act", lambda e: e.activation(ms[:], ms[:], AF.Sqrt, bias=epsc[:, 0:1], scale=1.0 / 128.0), reads=[ms, epsb], writes=[ms])
                    sc.op("dve", lambda e: e.reciprocal(rstd[:], ms[:]), reads=[ms], writes=[rstd])
                    for h in range(8):
                        sc.op("dve", lambda e, h=h: e.scalar_tensor_tensor(
                            oan[:, h, :], oat3[:, h, :], rstd[:, h:h + 1], sublg[:], ALU.mult, ALU.mult),
                            reads=[oat3, rstd, sublg], writes=[oan])
                    transpose_tm(lambda kc: oan[:, kc, :], oanT, oan)
                    for blk in range(2):
                        cs = slice(blk * 512, (blk + 1) * 512)
                        pga = nps()
                        mm_chain(pga, pga[:, :], [(xT[:, kc, t0:t0 + 128], w_ga[:, kc, cs]) for kc in range(8)], [xT, w_ga])
                        sc.op("dve", lambda e, pga=pga, cs=cs: e.tensor_tensor(sg[:], pga[:, :], b_ga[:, cs], ALU.add),
                              reads=[pga, b_ga], writes=[sg])
                        sc.op("act", lambda e: e.activation(sg[:], sg[:], AF.Sigmoid), reads=[sg], writes=[sg])
                        pya = nps()
                        mm_chain(pya, pya[:, :], [(oanT[:, ec, :], w_a_sb[:, ec, cs]) for ec in range(8)], [oanT, w_a_sb])
                        sc.op("dve", lambda e, pya=pya, cs=cs: e.tensor_tensor(mat[:, cs], pya[:, :], sg[:], ALU.mult),
                              reads=[pya, sg], writes=[mat])
                    sc.dma("sp", ma_sc[G(t0):G(t0) + 128, :], mat[:], [mat], [ma_sc], mat)
                sc.barrier()
                sc.end_scope(e3)

              with ExitStack() as e4:
                sb4 = lambda n, sh, dt=F32: sc.sbuf(n, sh, dt, e4)
                wstC = sb4("wstC2", [128, 8, 512])
                w_om = sb4("w_om", [128, 8, 1024], BF16)
                w_gb = sb4("w_gb", [128, 8, 1024], BF16)
                w_b_sb = sb4("w_b_sb", [128, 8, 1024], BF16)
                w_o_sb = sb4("w_o_sb", [128, 8, 1024], BF16)
                b_om = sb4("b_om", [128, 1024]); b_gb = sb4("b_gb", [128, 1024])
                mng = sb4("mng", [128, 1024]); l1g = sb4("l1g", [128, 1024]); l1b = sb4("l1b", [128, 1024])
                hht = sb4("hht", [128, 1024]); xt = sb4("xt", [128, 1024]); mat = sb4("mat2", [128, 1024])
                hn = sb4("hn", [128, 1024]); so = sb4("so", [128, 512]); mg = sb4("mg", [128, 1024])
                hbT = sb4("hbT", [128, 8, 128], BF16); mT = sb4("mT", [128, 8, 128], BF16)
                st = sb4("st", [128, 4, 6]); mv = sb4("mv", [128, 4, 2]); rs4 = sb4("rs4", [128, 4])
                st2 = sb4("st2", [128, 2, 6]); mv2 = sb4("mv2", [128, 2]); rs1 = sb4("rs1", [128, 1])
                load_w2(wstC, w_om, w_in, 5120)
                load_w2(wstC, w_gb, w_in, 7176)
                load_w2(wstC, w_b_sb, wb_d, 0)
                load_w2(wstC, w_o_sb, wo_d, 0)
                sc.dma("sp", b_om[:], b_in[0:1, 5120:6144].partition_broadcast(128), [b_in], [b_om], b_om)
                sc.dma("sp", b_gb[:], b_in[0:1, 7176:8200].partition_broadcast(128), [b_in], [b_gb], b_gb)
                sc.dma("sp", mng[:], vec_d[0:1, :].partition_broadcast(128), [vec_d], [mng], mng)
                sc.dma("sp", l1g[:], vec_d[1:2, :].partition_broadcast(128), [vec_d], [l1g], l1g)
                sc.dma("sp", l1b[:], vec_d[2:3, :].partition_broadcast(128), [vec_d], [l1b], l1b)
                for ti in range(NTT):
                    t0 = tok0[ti]
                    sc.dma("sp", hht[:], h_out[G(t0):G(t0) + 128, :], [h_out], [hht], hht)
                    sc.dma("sp", xt[:], xin[G(t0):G(t0) + 128, :], [xin], [xt], xt)
                    sc.dma("sp", mat[:], ma_sc[G(t0):G(t0) + 128, :], [ma_sc], [mat], mat)
                    for h in range(4):
                        sc.op("dve", lambda e, h=h: e.bn_stats(st[:, h, :], hht[:, h * 256:(h + 1) * 256]), reads=[hht], writes=[st])
                        sc.op("dve", lambda e, h=h: e.bn_aggr(mv[:, h, :], st[:, h, :]), reads=[st], writes=[mv])
                    sc.op("act", lambda e: e.activation(rs4[:], mv[:, :, 1], AF.Sqrt, bias=epsc[:, 0:1]), reads=[mv, epsb], writes=[rs4])
                    sc.op("dve", lambda e: e.reciprocal(rs4[:], rs4[:]), reads=[rs4], writes=[rs4])
                    for h in range(4):
                        sc.op("dve", lambda e, h=h: e.tensor_scalar(
                            hn[:, h * 256:(h + 1) * 256], hht[:, h * 256:(h + 1) * 256], mv[:, h, 0:1], rs4[:, h:h + 1],
                            ALU.subtract, ALU.mult), reads=[hht, mv, rs4], writes=[hn])
                    sc.op("dve", lambda e: e.tensor_tensor(hn[:], hn[:], mng[:], ALU.mult), reads=[hn, mng], writes=[hn])
                    for blk in range(2):
                        cs = slice(blk * 512, (blk + 1) * 512)
                        pom = nps()
                        mm_chain(pom, pom[:, :], [(xT[:, kc, t0:t0 + 128], w_om[:, kc, cs]) for kc in range(8)], [xT, w_om])
                        sc.op("dve", lambda e, pom=pom, cs=cs: e.tensor_tensor(so[:], pom[:, :], b_om[:, cs], ALU.add),
                              reads=[pom, b_om], writes=[so])
                        sc.op("act", lambda e: e.activation(so[:], so[:], AF.Sigmoid), reads=[so], writes=[so])
                        sc.op("dve", lambda e, cs=cs: e.tensor_tensor(hn[:, cs], hn[:, cs], so[:], ALU.mult), reads=[hn, so], writes=[hn])
                    transpose_tm(lambda kc: hn[:, kc * 128:(kc + 1) * 128], hbT, hn)
                    for blk in range(2):
                        cs = slice(blk * 512, (blk + 1) * 512)
                        pgb = nps()
                        mm_chain(pgb, pgb[:, :], [(xT[:, kc, t0:t0 + 128], w_gb[:, kc, cs]) for kc in range(8)], [xT, w_gb])
                        sc.op("dve", lambda e, pgb=pgb, cs=cs: e.tensor_tensor(so[:], pgb[:, :], b_gb[:, cs], ALU.add),
                              reads=[pgb, b_gb], writes=[so])
                        sc.op("act", lambda e: e.activation(so[:], so[:], AF.Sigmoid), reads=[so], writes=[so])
                        pyb = nps()
                        mm_chain(pyb, pyb[:, :], [(hbT[:, ec, :], w_b_sb[:, ec, cs]) for ec in range(8)], [hbT, w_b_sb])
                        sc.op("dve", lambda e, pyb=pyb, cs=cs: e.tensor_tensor(mg[:, cs], pyb[:, :], so[:], ALU.mult),
                              reads=[pyb, so], writes=[mg])
                        sc.op("dve", lambda e, cs=cs: e.tensor_tensor(mg[:, cs], mg[:, cs], mat[:, cs], ALU.add), reads=[mg, mat], writes=[mg])
                    transpose_tm(lambda kc: mg[:, kc * 128:(kc + 1) * 128], mT, mg)
                    for blk in range(2):
                        cs = slice(blk * 512, (blk + 1) * 512)
                        po = nps()
                        mm_chain(po, po[:, :], [(mT[:, ec, :], w_o_sb[:, ec, cs]) for ec in range(8)], [mT, w_o_sb])
                        sc.op("dve", lambda e, po=po, cs=cs: e.scalar_tensor_tensor(
                            hn[:, cs], xt[:, cs], ALPHA, po[:, :], ALU.mult, ALU.add), reads=[xt, po], writes=[hn])
                        sc.op("dve", lambda e, blk=blk, cs=cs: e.bn_stats(st2[:, blk, :], hn[:, cs]), reads=[hn], writes=[st2])
                    sc.op("dve", lambda e: e.bn_aggr(mv2[:], st2[:].rearrange("p a b -> p (a b)")), reads=[st2], writes=[mv2])
                    sc.op("act", lambda e: e.activation(rs1[:], mv2[:, 1:2], AF.Sqrt, bias=epsc[:, 0:1]), reads=[mv2, epsb], writes=[rs1])
                    sc.op("dve", lambda e: e.reciprocal(rs1[:], rs1[:]), reads=[rs1], writes=[rs1])
                    sc.op("dve", lambda e: e.tensor_scalar(hn[:], hn[:], mv2[:, 0:1], rs1[:, 0:1], ALU.subtract, ALU.mult),
                          reads=[hn, mv2, rs1], writes=[hn])
                    sc.op("dve", lambda e: e.tensor_tensor(hn[:], hn[:], l1g[:], ALU.mult), reads=[hn, l1g], writes=[hn])
                    sc.op("dve", lambda e: e.tensor_tensor(hn[:], hn[:], l1b[:], ALU.add), reads=[hn, l1b], writes=[hn])
                    sc.dma("sp", h1_sc[G(t0):G(t0) + 128, :], hn[:], [hn], [h1_sc], hn)
                sc.barrier()
                sc.end_scope(e4)
            if cfg.phaseD:
              with ExitStack() as e5:
                sb5 = lambda n, sh, dt=F32: sc.sbuf(n, sh, dt, e5)
                wpq = sb5("wpq", [128, 8, 2048])
                keysT = sb5("keysT", [128, 16, 128])
                kst = sb5("kst", [128, 128])
                l2g = sb5("l2g", [128, 1024]); l2b = sb5("l2b", [128, 1024])
                h1t = sb5("h1t", [128, 1024]); h1T = sb5("h1T", [128, 8, 128])
                qTj = [sb5(f"qTj{i}", [128, 128]) for i in range(2)]
                s_all = sb5("s_all", [128, 16, 128]); s2 = sb5("s2", [128, 128])
                sv = sb5("sv", [128, 16, 16]); si = sb5("si", [128, 16, 16], U32); sif = sb5("sif", [128, 16, 16])
                cand = sb5("cand", [128, 8, 256]); c2 = sb5("c2", [128, 256])
                top = sb5("top", [128, 8, 16]); pos = sb5("pos", [128, 8, 16], U32)
                pi_ = sb5("pi_", [128, 8, 16], U32); pj_ = sb5("pj_", [128, 8, 16], U32)
                pif = sb5("pif", [128, 8, 16]); pjf = sb5("pjf", [128, 8, 16])
                oh = sb5("oh", [128, 16, 16])
                e0 = sb5("e0", [128, 8, 16]); e1_ = sb5("e1_", [128, 8, 16]); ef = sb5("ef", [128, 8, 16])
                ei = sb5("ei", [128, 128], I32)
                nm = sb5("nm", [128, 8]); gexp = sb5("gexp", [128, 8, 16]); gsum = sb5("gsum", [128, 8]); rg = sb5("rg", [128, 8])
                gat = sb5("gat", [128, 8, 16])
                av = sb5("av", [128, 128]); wv_ = sb5("wv_", [128, 128])
                junk = sb5("junk", [128, 1024]); accv = sb5("accv", [128, 1024])
                NSL = 6
                Ug = [sb5(f"Ug{i}", [128, 1024], BF16) for i in range(NSL)]
                Vg = [sb5(f"Vg{i}", [128, 1024], BF16) for i in range(NSL)]
                if jb == 0:
                    it = 0
                    for (src_d, dst_d) in ((pu_d, pub_d), (pv_d, pvb_d)):
                        for r in range(128):
                            stg = junk if it % 2 == 0 else accv
                            cb = Ug[it % NSL]
                            sc.dma("sp", stg[:], src_d[r * 128:(r + 1) * 128, :], [src_d], [stg], stg)
                            if it % 3 == 0:
                                sc.op("act", lambda e, cb=cb, stg=stg: e.copy(cb[:], stg[:]), reads=[stg], writes=[cb])
                            elif it % 3 == 1:
                                sc.op("dve", lambda e, cb=cb, stg=stg: e.tensor_copy(cb[:], stg[:]), reads=[stg], writes=[cb])
                            else:
                                sc.op("pool", lambda e, cb=cb, stg=stg: e.tensor_copy(cb[:], stg[:]), reads=[stg], writes=[cb])
                            sc.dma("sp", dst_d[r * 128:(r + 1) * 128, :], cb[:], [cb], [dst_d], cb)
                            it += 1
                st2 = sb5("st2d", [128, 2, 6]); mv2 = sb5("mv2d", [128, 2]); rs1 = sb5("rs1d", [128, 1])
                IO = cst[:, 24:26, :].rearrange("p a (k i) -> p (a k) i", i=16)
                sc.dma("sp", wpq[:], wpq_d[:, :].rearrange("(kc p) c -> p kc c", p=128), [wpq_d], [wpq], wpq)
                sc.dma("sp", l2g[:], vec_d[3:4, :].partition_broadcast(128), [vec_d], [l2g], l2g)
                sc.dma("sp", l2b[:], vec_d[4:5, :].partition_broadcast(128), [vec_d], [l2b], l2b)
                for j in range(16):
                    sc.dma("sp", kst[:], pk_d[j], [pk_d], [kst], kst)
                    pb = nps()
                    sc.op("pe", lambda e, pb=pb: e.transpose(pb[:, 0:128], kst[:], ident), reads=[kst, cst], writes=[pb])
                    sc.op("act", lambda e, pb=pb, j=j: e.copy(keysT[:, j, :], pb[:, 0:128]), reads=[pb], writes=[keysT])
                h1src = h1in_d if cfg.peer_only else h1_sc
                for ti in range(NTT):
                    t0 = tok0[ti]
                    sc.dma("sp", h1t[:], h1src[G(t0):G(t0) + 128, :], [h1src], [h1t], h1t)
                    for half in range(2):
                        pb = nps()
                        for j in range(4):
                            kc = half * 4 + j
                            sc.op("pe", lambda e, pb=pb, j=j, kc=kc: e.transpose(pb[:, j * 128:(j + 1) * 128], h1t[:, kc * 128:(kc + 1) * 128], ident),
                                  reads=[h1t, cst], writes=[pb] if j == 0 else [], chain=[] if j == 0 else [pb])
                        sc.op("act", lambda e, pb=pb, half=half: e.copy(h1T[:, half * 4:half * 4 + 4, :], pb[:, :].rearrange("p (j r) -> p j r", j=4)),
                              reads=[pb], writes=[h1T])
                    for j in range(16):
                        pq = nps()
                        mm_chain(pq, pq[:, 0:128], [(wpq[:, kc, j * 128:(j + 1) * 128], h1T[:, kc, :]) for kc in range(8)], [wpq, h1T])
                        qt = qTj[j % 2]
                        sc.op("act", lambda e, pq=pq, qt=qt: e.copy(qt[:], pq[:, 0:128]), reads=[pq], writes=[qt])
                        pss = nps()
                        mm_chain(pss, pss[:, 0:128], [(qt[:], keysT[:, j, :])], [qt, keysT])
                        sc.op("dve", lambda e, pss=pss, j=j: e.tensor_copy(s_all[:, j, :], pss[:, 0:128]), reads=[pss], writes=[s_all])
                    for j in range(16):
                        sc.op("dve", lambda e, j=j: e.max(out=sv[:, j, 0:8], in_=s_all[:, j, :]), reads=[s_all], writes=[sv])
                        sc.op("dve", lambda e, j=j: e.match_replace(out=s2[:], in_to_replace=sv[:, j, 0:8], in_values=s_all[:, j, :], imm_value=-1e30),
                              reads=[sv, s_all], writes=[s2])
                        sc.op("dve", lambda e, j=j: e.max(out=sv[:, j, 8:16], in_=s2[:]), reads=[s2], writes=[sv])
                        sc.op("dve", lambda e, j=j: e.max_index(out=si[:, j, 0:8], in_max=sv[:, j, 0:8], in_values=s_all[:, j, :]),
                              reads=[sv, s_all], writes=[si])
                        sc.op("dve", lambda e, j=j: e.max_index(out=si[:, j, 8:16], in_max=sv[:, j, 8:16], in_values=s_all[:, j, :]),
                              reads=[sv, s_all], writes=[si])
                    sc.op("dve", lambda e: e.tensor_copy(sif[:], si[:]), reads=[si], writes=[sif])
                    for h in range(8):
                        sc.op("dve", lambda e, h=h: e.tensor_tensor(
                            cand[:, h, :].rearrange("p (i j) -> p i j", j=16),
                            sv[:, 2 * h, :].unsqueeze(2).to_broadcast([128, 16, 16]),
                            sv[:, 2 * h + 1, :].unsqueeze(1).to_broadcast([128, 16, 16]), ALU.add), reads=[sv], writes=[cand])
                    for h in range(8):
                        sc.op("dve", lambda e, h=h: e.max(out=top[:, h, 0:8], in_=cand[:, h, :]), reads=[cand], writes=[top])
                        sc.op("dve", lambda e, h=h: e.match_replace(out=c2[:], in_to_replace=top[:, h, 0:8], in_values=cand[:, h, :], imm_value=-1e30),
                              reads=[top, cand], writes=[c2])
                        sc.op("dve", lambda e, h=h: e.max(out=top[:, h, 8:16], in_=c2[:]), reads=[c2], writes=[top])
                        sc.op("dve", lambda e, h=h: e.max_index(out=pos[:, h, 0:8], in_max=top[:, h, 0:8], in_values=cand[:, h, :]),
                              reads=[top, cand], writes=[pos])
                        sc.op("dve", lambda e, h=h: e.max_index(out=pos[:, h, 8:16], in_max=top[:, h, 8:16], in_values=cand[:, h, :]),
                              reads=[top, cand], writes=[pos])
                    sc.op("dve", lambda e: e.tensor_scalar(nm[:], top[:, :, 0], -1.0, None, ALU.mult), reads=[top], writes=[nm])
                    for h in range(8):
                        sc.op("act", lambda e, h=h: e.activation(gexp[:, h, :], top[:, h, :], AF.Exp, bias=nm[:, h:h + 1]),
                              reads=[top, nm], writes=[gexp])
                    sc.op("dve", lambda e: e.tensor_reduce(gsum[:], gexp[:], AX.X, ALU.add), reads=[gexp], writes=[gsum])
                    sc.op("dve", lambda e: e.reciprocal(rg[:], gsum[:]), reads=[gsum], writes=[rg])
                    sc.op("dve", lambda e: e.tensor_tensor(gat[:], gexp[:], rg[:].unsqueeze(2).to_broadcast([128, 8, 16]), ALU.mult),
                          reads=[gexp, rg], writes=[gat])
                    sc.op("dve", lambda e: e.tensor_scalar(pi_[:], pos[:], 4, None, ALU.logical_shift_right), reads=[pos], writes=[pi_])
                    sc.op("dve", lambda e: e.tensor_scalar(pj_[:], pos[:], 15, None, ALU.bitwise_and), reads=[pos], writes=[pj_])
                    sc.op("dve", lambda e: e.tensor_copy(pif[:], pi_[:]), reads=[pi_], writes=[pif])
                    sc.op("dve", lambda e: e.tensor_copy(pjf[:], pj_[:]), reads=[pj_], writes=[pjf])
                    for h in range(8):
                        for (pf, jj, eo) in ((pif, 2 * h, e0), (pjf, 2 * h + 1, e1_)):
                            sc.op("dve", lambda e, pf=pf, h=h: e.tensor_tensor(
                                oh[:], pf[:, h, :].unsqueeze(2).to_broadcast([128, 16, 16]), IO, ALU.is_equal),
                                reads=[pf, cst], writes=[oh])
                            sc.op("dve", lambda e, jj=jj: e.tensor_tensor(
                                oh[:], oh[:], sif[:, jj, :].unsqueeze(1).to_broadcast([128, 16, 16]), ALU.mult),
                                reads=[oh, sif], writes=[oh])
                            sc.op("dve", lambda e, eo=eo, h=h: e.tensor_reduce(eo[:, h, :], oh[:], AX.X, ALU.add), reads=[oh], writes=[eo])
                    sc.op("dve", lambda e: e.scalar_tensor_tensor(ef[:], e0[:], 128.0, e1_[:], ALU.mult, ALU.add), reads=[e0, e1_], writes=[ef])
                    sc.op("dve", lambda e: e.tensor_copy(ei[:], ef[:].rearrange("p h k -> p (h k)")), reads=[ef], writes=[ei])
                    if cfg.debug:
                        sc.dma("sp", dbg_ei[G(t0):G(t0) + 128, :], ei[:], [ei], [dbg_ei], ei)
                        sc.dma("sp", dbg_g[G(t0):G(t0) + 128, :], gat[:].rearrange("p h k -> p (h k)"), [gat], [dbg_g], gat)
                    for hk in range(128):
                        ug = Ug[hk % NSL]
                        sc.dma("pool", ug[:], pub_d[:, :], [pub_d, ei], [ug], ug,
                               indirect=dict(out_offset=None, in_offset=bass.IndirectOffsetOnAxis(ap=ei[:, hk:hk + 1], axis=0)))
                        sc.op("dve", lambda e, ug=ug, hk=hk: e.scalar_tensor_tensor(
                            junk[:], ug[:], 1.0, h1t[:], ALU.mult, ALU.mult, accum_out=av[:, hk:hk + 1]),
                            reads=[ug, h1t], writes=[junk, av])
                    sc.op("act", lambda e: e.activation(wv_[:], av[:], AF.Gelu), reads=[av], writes=[wv_])
                    sc.op("dve", lambda e: e.tensor_tensor(wv_[:], wv_[:], gat[:].rearrange("p h k -> p (h k)"), ALU.mult),
                          reads=[wv_, gat], writes=[wv_])
                    for hk in range(128):
                        vg = Vg[hk % NSL]
                        sc.dma("pool", vg[:], pvb_d[:, :], [pvb_d, ei], [vg], vg,
                               indirect=dict(out_offset=None, in_offset=bass.IndirectOffsetOnAxis(ap=ei[:, hk:hk + 1], axis=0)))
                        if hk == 0:
                            sc.op("dve", lambda e, vg=vg: e.tensor_scalar(accv[:], vg[:], wv_[:, 0:1], None, ALU.mult),
                                  reads=[vg, wv_], writes=[accv])
                        else:
                            sc.op("dve", lambda e, vg=vg, hk=hk: e.scalar_tensor_tensor(
                                accv[:], vg[:], wv_[:, hk:hk + 1], accv[:], ALU.mult, ALU.add), reads=[vg, wv_, accv], writes=[accv])
                    sc.op("dve", lambda e: e.scalar_tensor_tensor(accv[:], h1t[:], ALPHA, accv[:], ALU.mult, ALU.add),
                          reads=[h1t, accv], writes=[accv])
                    for blk in range(2):
                        sc.op("dve", lambda e, blk=blk: e.bn_stats(st2[:, blk, :], accv[:, blk * 512:(blk + 1) * 512]), reads=[accv], writes=[st2])
                    sc.op("dve", lambda e: e.bn_aggr(mv2[:], st2[:].rearrange("p a b -> p (a b)")), reads=[st2], writes=[mv2])
                    sc.op("act", lambda e: e.activation(rs1[:], mv2[:, 1:2], AF.Sqrt, bias=epsc[:, 0:1]), reads=[mv2, epsb], writes=[rs1])
                    sc.op("dve", lambda e: e.reciprocal(rs1[:], rs1[:]), reads=[rs1], writes=[rs1])
                    sc.op("dve", lambda e: e.tensor_scalar(accv[:], accv[:], mv2[:, 0:1], rs1[:, 0:1], ALU.subtract, ALU.mult),
                          reads=[accv, mv2, rs1], writes=[accv])
                    sc.op("dve", lambda e: e.tensor_tensor(accv[:], accv[:], l2g[:], ALU.mult), reads=[accv, l2g], writes=[accv])
                    sc.op("dve", lambda e: e.tensor_tensor(accv[:], accv[:], l2b[:], ALU.add), reads=[accv, l2b], writes=[accv])
                    sc.dma("sp", y_out[G(t0):G(t0) + 128, :], accv[:], [accv], [y_out], accv, is_output=True)
                sc.barrier()
                sc.end_scope(e5)
        for jb in range(NPS):
            run_job(jb, jb == 0)
        sc.finish()
    return nc


def host_inputs(inp, c, cfg):
    S, NPS, NSQ = cfg.S, cfg.NPS, cfg.NSQ
    f = lambda k: np.asarray(inp[k], np.float32)
    xp, xs = f("x_prompt"), f("x_sample")
    DS = xs.shape[1]
    xin = np.zeros((NPS * S + 128, D), np.float32)
    xin[:NPS * S] = xp[c * NPS:(c + 1) * NPS].reshape(NPS * S, D)
    xin[NPS * S:NPS * S + NSQ * DS] = xs[c * NSQ:(c + 1) * NSQ].reshape(NSQ * DS, D)
    b_in = f("b_in")[0]
    smallc = np.zeros((128, 128), np.float32)
    smallc[:, 0:8] = b_in[3072:4096].reshape(8, 128).T
    wc = f("w_conv")[0]
    smallc[:, 8:40] = wc.reshape(4, 8, 128).transpose(2, 1, 0).reshape(128, 32)
    smallc[:, 40:48] = f("b_conv")[0].reshape(8, 128).T
    for j in range(NSQ):
        smallc[4 * j:4 * j + 4, 48 + j] = 1.0
    smallc[:, 56:72] = b_in[0:2048].reshape(16, 128).T
    smallc[0:64, 72:88] = b_in[0:1024].reshape(16, 64).T
    smallc[:, 88] = np.arange(128)
    sm = f("state_m")[0][c * NSQ:(c + 1) * NSQ]
    mst = np.zeros((128, 4 + 4 * NSQ), np.float32)
    for j in range(NSQ):
        mst[4 * j:4 * j + 4, 0:4] = sm[j]
        mst[:, 4 + 4 * j:8 + 4 * j] = sm[j]
    cv = f("state_conv")[0][c * NSQ:(c + 1) * NSQ]
    convT = np.ascontiguousarray(cv.reshape(NSQ, 3, 8, 128).transpose(3, 2, 0, 1))
    return {
        "xin": xin, "w_in": np.ascontiguousarray(f("w_in")[0]), "b_in": np.ascontiguousarray(b_in[None, :]),
        "consts": make_consts(NSQ), "smallc": smallc, "mst": mst,
        "w_a": np.ascontiguousarray(f("w_a")[0]), "w_b": np.ascontiguousarray(f("w_b")[0]), "w_o": np.ascontiguousarray(f("w_o")[0]),
        "subln_g": np.ascontiguousarray(f("subln_g")[0][None, :]),
        "vecs": np.stack([f("mnorm_g")[0], f("ln1_g")[0], f("ln1_b")[0], f("ln2_g")[0], f("ln2_b")[0],
                          np.zeros(D, np.float32), np.zeros(D, np.float32), np.zeros(D, np.float32)]),
        "w_pq": np.ascontiguousarray(f("w_pq")[0]), "p_keys": np.ascontiguousarray(f("p_keys")[0].reshape(16, 128, 128)),
        "p_u": np.ascontiguousarray(f("p_u")[0]), "p_v": np.ascontiguousarray(f("p_v")[0]),
        "lam": np.stack([f("lam_q1")[0], f("lam_k1")[0], f("lam_q2")[0], f("lam_k2")[0]]),
        "w_qm": np.ascontiguousarray(f("w_qm")[0]), "w_km": np.ascontiguousarray(f("w_km")[0]),
        "st_C": np.ascontiguousarray(f("state_C")[0][c * NSQ:(c + 1) * NSQ]),
        "st_n": np.ascontiguousarray(f("state_n")[0][c * NSQ:(c + 1) * NSQ]),
        "st_convT": convT,
        "cache_k": np.asarray(inp["cache_k"], np.float32)[0].reshape(-1, D),
        "cache_v": np.asarray(inp["cache_v"], np.float32)[0].reshape(-1, D),
        "ptab": np.ascontiguousarray(np.asarray(inp["page_table"], np.int32)[c * NSQ:(c + 1) * NSQ].reshape(1, -1)),
    }


def assemble(R, inp, cfg):
    S, NPS, NSQ = cfg.S, cfg.NPS, cfg.NSQ
    xs = np.asarray(inp["x_sample"])
    nco = len(R)
    DS = xs.shape[1]
    B, DB = nco * NPS, nco * NSQ

    def gather(name):
        p = np.concatenate([R[c][name][:NPS * S].reshape(NPS, S, D) for c in range(nco)])
        s = np.concatenate([R[c][name][NPS * S:NPS * S + NSQ * DS].reshape(NSQ, DS, D) for c in range(nco)])
        return p, s
    kp, ks = gather("k_out")
    vp, vs = gather("v_out")
    up, us = gather("conv_out")
    y_p, y_s = gather("y_out")
    k_p = kp.reshape(1, B, S, 8, 128); v_p = vp.reshape(1, B, S, 8, 128)
    k_s = ks.reshape(1, DB, DS, 8, 128); v_s = vs.reshape(1, DB, DS, 8, 128)
    conv_p = np.ascontiguousarray(up[:, S - 3:, :]).reshape(1, B, 3, D)
    conv_s = np.ascontiguousarray(us[:, DS - 3:, :]).reshape(1, DB, 3, D)
    cat = lambda name, sl: np.concatenate([R[c][name][sl] for c in range(nco)])[None]
    C_p, n_p, m_p = cat("C_out", slice(0, NPS)), cat("n_out", slice(0, NPS)), cat("m_out", slice(0, NPS))
    C_s, n_s, m_s = cat("C_out", slice(NPS, None)), cat("n_out", slice(NPS, None)), cat("m_out", slice(NPS, None))
    return (y_p, y_s, k_p, v_p, C_p, n_p, m_p, conv_p, k_s, v_s, C_s, n_s, m_s, conv_s)


def kernel(**inp):
    cfg = Cfg()
    cfg.NPOOL = int(np.asarray(inp["cache_k"]).shape[1])
    nc = build(cfg)
    in_maps = [host_inputs(inp, c, cfg) for c in range(cfg.NCORES)]
    res = run_bass_kernel_spmd(nc, in_maps, core_ids=list(range(cfg.NCORES)))
    return assemble(res.results, inp, cfg)
```
